# Optimizing a Trainium2 kernel written in Bass

```python
import math
import jax, jax.numpy as jnp
from jax import lax
import numpy as np

D_MODEL = 2048
BATCH = 4
SEQ = 2048
DEPTH = 1

N_MEM = 256
EPS = 1e-6
GLA_HEADS = 4
GLA_DK_TOTAL = D_MODEL // 2
GLA_DV_TOTAL = D_MODEL
GLA_DK = GLA_DK_TOTAL // GLA_HEADS
GLA_DV = GLA_DV_TOTAL // GLA_HEADS
GLA_RANK = 16
GLA_TAU = 16.0
GLA_CHUNK = 64
SB_HEAD_DIM = 128
SB_HEADS = D_MODEL // SB_HEAD_DIM
SB_WIDTH = SB_HEADS * SB_HEAD_DIM
SB_BLOCK = 128
MEM_HEADS = 4
MEM_WIDTH = D_MODEL
MEM_HEAD_DIM = MEM_WIDTH // MEM_HEADS
N_BRANCHES = 3

IN_SPLIT_SIZES = (
    GLA_DK_TOTAL,
    GLA_DK_TOTAL,
    GLA_DV_TOTAL,
    GLA_DV_TOTAL,
    GLA_RANK,
    SB_WIDTH,
    SB_WIDTH,
    SB_WIDTH,
    SB_WIDTH,
    MEM_WIDTH,
    N_BRANCHES * D_MODEL,
)
IN_TOTAL = sum(IN_SPLIT_SIZES)

kernel_name = "hybrid_gla_stickbreak_memory_layer"


def rmsnorm(x, g):
    xf = x.astype(jnp.float32)
    y = xf * lax.rsqrt(jnp.mean(xf * xf, axis=-1, keepdims=True) + EPS)
    return (y * g.astype(jnp.float32)).astype(x.dtype)


def split_heads(t, n_heads):
    b, s, w = t.shape
    return t.reshape(b, s, n_heads, w // n_heads).transpose(0, 2, 1, 3)


def merge_heads(t):
    b, h, s, d = t.shape
    return t.transpose(0, 2, 1, 3).reshape(b, s, h * d)


def gla_chunked(q, k, v, log_a):
    B, H, T, dk = q.shape
    dv = v.shape[-1]
    C = GLA_CHUNK
    N = T // C
    f32 = jnp.float32

    def to_chunks(t):
        return t.astype(f32).reshape(B, H, N, C, t.shape[-1]).transpose(2, 0, 1, 3, 4)

    qc, kc, vc, ac = (to_chunks(t) for t in (q, k, v, log_a))
    causal = jnp.tril(jnp.ones((C, C), dtype=bool))[:, :, None]

    def step(S, inp):
        qi, ki, vi, ai = inp
        b = jnp.cumsum(ai, axis=-2)
        o_inter = jnp.einsum('bhck,bhkv->bhcv', qi * jnp.exp(b), S)
        diff = b[:, :, :, None, :] - b[:, :, None, :, :]
        decay = jnp.exp(jnp.where(causal, diff, -jnp.inf))
        attn = jnp.sum(qi[:, :, :, None, :] * ki[:, :, None, :, :] * decay, axis=-1)
        o_intra = jnp.einsum('bhij,bhjv->bhiv', attn, vi)
        b_last = b[:, :, -1:, :]
        S_new = (jnp.exp(b_last[:, :, 0, :])[..., None] * S
                 + jnp.einsum('bhck,bhcv->bhkv', ki * jnp.exp(b_last - b), vi))
        return S_new, o_inter + o_intra

    S0 = jnp.zeros((B, H, dk, dv), f32)
    _, o = lax.scan(step, S0, (qc, kc, vc, ac))
    return o.transpose(1, 2, 0, 3, 4).reshape(B, H, T, dv).astype(v.dtype)


def stick_breaking(q, k, v):
    B, H, T, d = q.shape
    scale = 1.0 / math.sqrt(d)
    outs = []
    for blk in range(T // SB_BLOCK):
        q0 = blk * SB_BLOCK
        kend = q0 + SB_BLOCK
        qb = q[:, :, q0:kend]
        kb = k[:, :, :kend]
        vb = v[:, :, :kend]
        z = jnp.einsum('bhtd,bhsd->bhts', qb, kb).astype(jnp.float32) * scale
        t_idx = q0 + jnp.arange(SB_BLOCK)[:, None]
        s_idx = jnp.arange(kend)[None, :]
        strict = s_idx < t_idx
        log_1m_beta = jnp.where(strict, jax.nn.log_sigmoid(-z), 0.0)
        after = lax.cumsum(log_1m_beta, axis=3, reverse=True) - log_1m_beta
        log_w = jax.nn.log_sigmoid(z) + after
        w = jnp.where(strict, jnp.exp(log_w), 0.0)
        outs.append(jnp.einsum('bhts,bhsd->bhtd', w.astype(v.dtype), vb))
    return jnp.concatenate(outs, axis=2)


def memory_attention(q, k, v):
    d = q.shape[-1]
    s = jnp.einsum('bhtd,bhmd->bhtm', q, k).astype(jnp.float32) / math.sqrt(d)
    p = jax.nn.softmax(s, axis=-1).astype(v.dtype)
    return jnp.einsum('bhtm,bhmd->bhtd', p, v)


def setup_inputs(seed: int = 0) -> dict:
    key = jax.random.key(seed)
    ks = jax.random.split(key, 16)
    f32 = jnp.float32
    D = D_MODEL
    nrm = lambda k, shape, fan_in: jax.random.normal(k, shape, f32) * (fan_in ** -0.5)
    gain = lambda k, n: 1.0 + 0.02 * jax.random.normal(k, (n,), f32)
    return {
        "x": jax.random.normal(ks[0], (BATCH, SEQ, D), f32),
        "mem": jax.random.normal(ks[1], (BATCH, N_MEM, D), f32),
        "norm_pre_g": gain(ks[2], D),
        "norm_post_g": gain(ks[3], D),
        "norm_mem_g": gain(ks[4], D),
        "w_in": nrm(ks[5], (D, IN_TOTAL), D),
        "gla_a_w2": nrm(ks[6], (GLA_RANK, GLA_DK_TOTAL), GLA_RANK),
        "gla_a_b": 0.1 * jax.random.normal(ks[7], (GLA_DK_TOTAL,), f32),
        "gla_head_norm_g": gain(ks[8], GLA_DV),
        "w_mem_kv": nrm(ks[9], (D, 2 * MEM_WIDTH), D),
        "w_proj_gla": nrm(ks[10], (GLA_DV_TOTAL, D), GLA_DV_TOTAL),
        "w_proj_sb": nrm(ks[11], (SB_WIDTH, D), SB_WIDTH),
        "w_proj_mem": nrm(ks[12], (MEM_WIDTH, D), MEM_WIDTH),
        "w_out": nrm(ks[13], (D, D), D),
    }


def reference(x, mem, norm_pre_g, norm_post_g, norm_mem_g, w_in, gla_a_w2, gla_a_b,
              gla_head_norm_g, w_mem_kv, w_proj_gla, w_proj_sb, w_proj_mem, w_out):
    B, T, D = x.shape
    mem_h = rmsnorm(mem, norm_mem_g)
    mk, mv = jnp.split(mem_h @ w_mem_kv, 2, axis=-1)
    mk, mv = split_heads(mk, MEM_HEADS), split_heads(mv, MEM_HEADS)

    for _ in range(DEPTH):
        h = rmsnorm(x, norm_pre_g)
        proj = h @ w_in
        idx = np.cumsum(IN_SPLIT_SIZES)[:-1].tolist()
        (gq, gk, gv, gg, ga, sq, sk, sv, sg, mq, gates) = jnp.split(proj, idx, axis=-1)

        log_a = jax.nn.log_sigmoid(
            (ga @ gla_a_w2 + gla_a_b).astype(jnp.float32)) / GLA_TAU
        o_a = gla_chunked(split_heads(gq * (GLA_DK ** -0.5), GLA_HEADS),
                          split_heads(gk, GLA_HEADS),
                          split_heads(gv, GLA_HEADS),
                          split_heads(log_a, GLA_HEADS))
        o_a = merge_heads(rmsnorm(o_a, gla_head_norm_g)) * jax.nn.silu(gg)

        o_b = stick_breaking(split_heads(sq, SB_HEADS),
                             split_heads(sk, SB_HEADS),
                             split_heads(sv, SB_HEADS))
        o_b = merge_heads(o_b) * jax.nn.silu(sg)

        o_m = merge_heads(memory_attention(split_heads(mq, MEM_HEADS), mk, mv))

        g_a, g_b, g_m = jnp.split(jax.nn.sigmoid(gates), N_BRANCHES, axis=-1)
        merged = (g_a * (o_a @ w_proj_gla)
                  + g_b * (o_b @ w_proj_sb)
                  + g_m * (o_m @ w_proj_mem))
        y = merged @ w_out
        x = x + rmsnorm(y, norm_post_g)
    return x
```

```python
import math
from contextlib import ExitStack

import numpy as np
import concourse.bass as bass
import concourse.mybir as mybir
from concourse.bass_utils import run_bass_kernel_spmd

F32 = mybir.dt.float32
BF16 = mybir.dt.bfloat16
AF = mybir.ActivationFunctionType
ALU = mybir.AluOpType
AX = mybir.AxisListType

D = 2048
TOWN = 1024
TPRE = 1024
NMEM = 256
EPS = 1e-6
IN_TOTAL = 22544
C_GQ, C_GK, C_GV, C_GG, C_GA = 0, 1024, 2048, 4096, 6144
C_SQ, C_SK, C_SV, C_SG = 6160, 8208, 10256, 12304
C_MQ, C_GATE = 14352, 16400

SAME_ENGINE_SYNC = True
STRICT_SAME_ENGINE = True
ENGS = ("pe", "act", "dve", "pool", "sp")


class Prog:
    def __init__(self):
        self.ops = []
        self.lastw = {}
        self.readers = {}
        self.dma_cnt = {}
        self.barrier_deps = {e: set() for e in ENGS}
        self.pe_exempt = False

    def op(self, eng, fn, reads=(), writes=(), dma_slot=None, nofence=False):
        idx = len(self.ops)
        deps = set()
        raw = set()
        for k in reads:
            w = self.lastw.get(k)
            if w is not None:
                deps.add(w)
                raw.add(w)
        for k in writes:
            w = self.lastw.get(k)
            if w is not None:
                deps.add(w)
            deps.update(self.readers.get(k, ()))
        if not nofence and not (eng == "pe" and self.pe_exempt):
            deps |= self.barrier_deps[eng]
            raw |= self.barrier_deps[eng]
            self.barrier_deps[eng] = set()
        rec = dict(eng=eng, fn=fn, deps=deps, raw=raw, dma=None, signal=False)
        if dma_slot is not None:
            c = self.dma_cnt.get(dma_slot, 0) + 1
            self.dma_cnt[dma_slot] = c
            rec["dma"] = (dma_slot, 16 * c)
        self.ops.append(rec)
        for k in reads:
            self.readers.setdefault(k, []).append(idx)
        for k in writes:
            self.lastw[k] = idx
            self.readers[k] = []
        return idx

    def barrier(self, full=False):
        self.pe_exempt = not full
        last = {}
        for i, o in enumerate(self.ops):
            if o["dma"] is not None:
                last[("dma", o["dma"][0])] = i
            else:
                last[o["eng"]] = i
        deps = set(last.values())
        for e in ENGS:
            self.barrier_deps[e] = set(deps)

    def emit(self, nc, stack):
        ops = self.ops
        for i, o in enumerate(ops):
            for d in o["deps"]:
                p = ops[d]
                if p["dma"] is not None:
                    continue
                if p["eng"] == o["eng"]:
                    if p["eng"] == "pe":
                        continue
                    if not (SAME_ENGINE_SYNC and (STRICT_SAME_ENGINE or d in o["raw"])):
                        continue
                p["signal"] = True
        cnt = {e: 0 for e in ENGS}
        for o in ops:
            if o["dma"] is None and o["signal"]:
                cnt[o["eng"]] += 1
                o["sig"] = (o["eng"], cnt[o["eng"]])
            elif o["dma"] is not None:
                o["sig"] = ("dma:" + o["dma"][0], o["dma"][1])
        sems = {}
        for e in ENGS:
            sems[e] = stack.enter_context(nc.semaphore("s_" + e))
        for slot in self.dma_cnt:
            sems["dma:" + slot] = stack.enter_context(nc.semaphore("d_" + slot))
        seen = {e: {} for e in ENGS}
        for o in ops:
            e = o["eng"]
            waits = {}
            for d in o["deps"]:
                p = ops[d]
                if p["dma"] is None:
                    if p["eng"] == e:
                        if e == "pe" or not (SAME_ENGINE_SYNC and (STRICT_SAME_ENGINE or d in o["raw"])):
                            continue
                s, v = p["sig"]
                if seen[e].get(s, 0) >= v:
                    continue
                waits[s] = max(waits.get(s, 0), v)
            for s, v in waits.items():
                seen[e][s] = v
            o["waits"] = sorted(waits.items())
        block = stack.enter_context(nc.Block())
        nwaits = sum(len(o["waits"]) for o in ops)
        self.stats = dict(n_ops=len(ops), n_waits=nwaits,
                          per_eng={e: sum(1 for o in ops if o["eng"] == e) for e in ENGS})

        def run(engname, eng):
            for o in ops:
                if o["eng"] != engname:
                    continue
                for s, v in o["waits"]:
                    eng.wait_ge(sems[s], v)
                ins = o["fn"](eng)
                if ins is None:
                    continue
                if o["dma"] is not None:
                    ins.then_inc(sems["dma:" + o["dma"][0]], 16)
                elif o["signal"]:
                    ins.then_inc(sems[engname], 1)

        @block.tensor
        def _(eng):
            run("pe", eng)

        @block.scalar
        def _(eng):
            run("act", eng)

        @block.vector
        def _(eng):
            run("dve", eng)

        @block.gpsimd
        def _(eng):
            run("pool", eng)

        @block.sync
        def _(eng):
            run("sp", eng)


def build_program(dbg=None, stop_after=99):
    nc = bass.Bass("TRN2", target_bir_lowering=False)
    P = Prog()
    stack = ExitStack()

    def din(name, shape):
        return nc.dram_tensor(name, list(shape), F32, kind="ExternalInput").ap()

    xo = din("xo", (TOWN, D))
    xp = din("xp", (TPRE, D))
    memb = din("memb", (NMEM, D))
    g_pre_d = din("norm_pre_g", (D,))
    g_post_d = din("norm_post_g", (D,))
    g_mem_d = din("norm_mem_g", (D,))
    w_in = din("w_in", (D, IN_TOTAL))
    w2_d = din("gla_a_w2", (16, 1024))
    ab_d = din("gla_a_b", (1024,))
    ghead_d = din("gla_head_norm_g", (512,))
    w_mem_kv = din("w_mem_kv", (D, 4096))
    w_pa = din("w_proj_gla", (D, D))
    w_pb = din("w_proj_sb", (D, D))
    w_pm = din("w_proj_mem", (D, D))
    w_out = din("w_out", (D, D))
    out_d = nc.dram_tensor("out", [TOWN, D], F32, kind="ExternalOutput").ap()
    dbg_d = {}
    if dbg:
        for name, shape in dbg.items():
            dbg_d[name] = nc.dram_tensor("dbg_" + name, list(shape), F32, kind="ExternalOutput").ap()

    def sb(name, shape, dt):
        return stack.enter_context(nc.sbuf_tensor(name, list(shape), dt))

    hTp = sb("hTp", (128, 16, TPRE), BF16)
    hTo = sb("hTo", (128, 16, TOWN), BF16)
    oT = sb("oT", (128, 16, TOWN), BF16)
    wsl = sb("wsl", (128, 4, 16, 256), BF16)
    ARENA_B = 72 * 1024
    arena = sb("arena", (128, ARENA_B // 2), BF16)
    ident = sb("ident", (128, 128), BF16)
    triS = sb("triS", (128, 128), BF16)
    triI = sb("triI", (128, 128), BF16)
    ones = sb("ones", (128, 128), BF16)
    maskS = sb("maskS", (128, 128), F32)
    maskI = sb("maskI", (128, 128), F32)
    gpre = sb("gpre", (128, 16), F32)
    gmem = sb("gmem", (128, 16), F32)
    ghead = sb("ghead", (128, 512), F32)
    w2aug = sb("w2aug", (17, 1024), BF16)
    wga = sb("wga", (128, 16, 16), BF16)
    small = sb("small", (128, 64), F32)
    ps = stack.enter_context(nc.psum_tensor("ps", [128, 8, 512], F32))

    def bank(b):
        return ps[:, b, :]

    def bank_bf(b, nb=1):
        return ps[:, b:b + nb, :].rearrange("p b n -> p (b n)").bitcast(BF16)

    class Arena:
        def __init__(self, base, nbytes):
            self.base, self.nbytes, self.off = base, nbytes, 0

        def reset(self, off=0):
            self.off = off

        def take(self, shape, dt):
            esz = 4 if dt == F32 else 2
            n = int(np.prod(shape[1:]))
            nb = (n * esz + 31) // 32 * 32
            assert self.off + nb <= self.nbytes, ("arena overflow", self.off, nb, self.nbytes)
            v = self.base[0:shape[0], self.off // 2:(self.off + n * esz) // 2]
            self.off += nb
            if dt == F32:
                v = v.bitcast(F32)
            if len(shape) == 3:
                v = v.rearrange("p (a b) -> p a b", b=shape[2])
            return v

    A = Arena(arena, ARENA_B)
    hTp_flat = hTp[:].rearrange("p a b -> p (a b)")
    A2 = Arena(hTp_flat, 32 * 1024)

    def hT(g0, n):
        t = g0 // 128
        keys = ["hT%d" % i for i in range(t, (g0 + n + 127) // 128)]
        if g0 < TPRE:
            return hTp[:, :, g0:g0 + n], keys
        return hTo[:, :, g0 - TPRE:g0 - TPRE + n], keys

    def mm(out, lhsT, rhs, start, stop, reads, writes, sgc=False):
        P.op("pe", lambda e: e.matmul(out, lhsT=lhsT, rhs=rhs, start=start, stop=stop, skip_group_check=sgc),
             reads, writes)

    def tr(out, in_, reads, writes):
        P.op("pe", lambda e: e.transpose(out=out, in_=in_, identity=ident[:]), list(reads) + ["const"], writes)

    def act(out, in_, func, reads, writes, bias=None, scale=None, accum=None):
        kw = {}
        if bias is not None:
            kw["bias"] = bias
        if scale is not None:
            kw["scale"] = scale
        if accum is not None:
            kw["accum_out"] = accum
        P.op("act", lambda e: e.activation(out=out, in_=in_, func=func, **kw), reads, writes)

    def tt(eng, out, in0, in1, op, reads, writes):
        P.op(eng, lambda e: e.tensor_tensor(out=out, in0=in0, in1=in1, op=op), reads, writes)

    def stt(out, in0, scalar, in1, op0, op1, reads, writes):
        P.op("dve", lambda e: e.scalar_tensor_tensor(out=out, in0=in0, scalar=scalar, in1=in1, op0=op0, op1=op1),
             reads, writes)

    def ts(eng, out, in0, s1, op0, reads, writes, s2=None, op1=None):
        if op1 is None:
            P.op(eng, lambda e: e.tensor_scalar(out=out, in0=in0, scalar1=s1, scalar2=None, op0=op0), reads, writes)
        else:
            P.op(eng, lambda e: e.tensor_scalar(out=out, in0=in0, scalar1=s1, scalar2=s2, op0=op0, op1=op1),
                 reads, writes)

    def cp(eng, out, in_, reads, writes):
        if eng == "act":
            P.op("act", lambda e: e.copy(out=out, in_=in_), reads, writes)
        else:
            P.op(eng, lambda e: e.tensor_copy(out=out, in_=in_), reads, writes)

    def dma(eng, out, in_, slot, reads, writes, nofence=False, **kw):
        P.op(eng, lambda e: e.dma_start(out=out, in_=in_, **kw), reads, writes, dma_slot=slot, nofence=nofence)

    wslot_rr = [0]

    def load_w(wd, col0, ncols=256, slot=None):
        if slot is None:
            slot = wslot_rr[0] % 4
            wslot_rr[0] += 1
        key = "wsl%d" % slot
        dma("pool", wsl[:, slot, :, 0:ncols], wd[:, col0:col0 + ncols].rearrange("(c p) n -> p c n", p=128),
            key, [], [key], nofence=True)
        return wsl[:, slot], key

    def pool_const(t, val_true_cmp, fill_else):
        pass

    P.op("pool", lambda e: e.memset(ident[:], 0.0), [], ["c_ident"])
    P.op("pool", lambda e: e.affine_select(out=ident[:], in_=ident[:], pattern=[[-1, 128]], compare_op=ALU.not_equal,
                                           fill=1.0, base=0, channel_multiplier=1), ["c_ident"], ["c_ident"])
    P.op("pool", lambda e: e.memset(ones[:], 1.0), [], ["c_ones"])
    P.op("pool", lambda e: e.affine_select(out=triS[:], in_=ones[:], pattern=[[-1, 128]], compare_op=ALU.is_gt,
                                           fill=0.0, base=0, channel_multiplier=1), ["c_ones"], ["c_triS"])
    P.op("pool", lambda e: e.affine_select(out=triI[:], in_=ones[:], pattern=[[1, 128]], compare_op=ALU.is_ge,
                                           fill=0.0, base=0, channel_multiplier=-1), ["c_ones"], ["c_triI"])
    P.op("pool", lambda e: e.affine_select(out=maskS[:], in_=ones[:], pattern=[[1, 128]], compare_op=ALU.is_gt,
                                           fill=0.0, base=0, channel_multiplier=-1), ["c_ones"], ["c_maskS"])
    P.op("pool", lambda e: e.affine_select(out=maskI[:], in_=ones[:], pattern=[[1, 128]], compare_op=ALU.is_ge,
                                           fill=0.0, base=0, channel_multiplier=-1), ["c_ones"], ["c_maskI"])
    dma("sp", gpre[:], g_pre_d.rearrange("(c p) -> p c", p=128), "c0", [], ["c_gpre"], allow_slow_non_contiguous=True)
    dma("sp", gmem[:], g_mem_d.rearrange("(c p) -> p c", p=128), "c1", [], ["c_gmem"], allow_slow_non_contiguous=True)
    dma("sp", ghead[:], ghead_d.partition_broadcast(128), "c2", [], ["c_ghead"])
    dma("pool", w2aug[0:16, :], w2_d, "c3", [], ["c_w2a"])
    dma("pool", w2aug[16:17, :], ab_d.rearrange("(o n) -> o n", o=1), "c4", [], ["c_w2b"])
    dma("pool", wga[:], w_in[:, C_GA:C_GA + 16].rearrange("(c p) n -> p c n", p=128), "c5", [], ["c_wga"])
    CONST = ["c_ident", "c_ones", "c_triS", "c_triI", "c_maskS", "c_maskI", "c_gpre", "c_gmem", "c_ghead",
             "c_w2a", "c_w2b", "c_wga"]
    P.barrier(full=True)

    def dump(name, src_ap, reads, rows=None):
        if name not in dbg_d:
            return
        dst = dbg_d[name]
        P.barrier(full=True)
        stg_full = arena[:, ARENA_B // 2 - 4096:ARENA_B // 2].bitcast(F32)
        n = int(np.prod(src_ap.shape[1:]))
        np_ = src_ap.shape[0]
        assert n <= 2048
        stg = stg_full[0:np_, 0:n]
        src2 = src_ap if len(src_ap.shape) == 2 else src_ap.rearrange("p a b -> p (a b)")
        if len(src_ap.shape) == 3:
            stg_v = stg.rearrange("p (a b) -> p a b", b=src_ap.shape[2])
            P.op("dve", lambda e: e.tensor_copy(out=stg_v, in_=src_ap), ["dbgstg"], ["dbgstg"])
        else:
            P.op("dve", lambda e: e.tensor_copy(out=stg, in_=src2), ["dbgstg"], ["dbgstg"])
        dma("sp", dst, stg, "dbg_" + name, ["dbgstg"], ["dbgout"])
        P.barrier(full=True)

    A.reset()
    xs = [A.take((128, D), F32) for _ in range(2)]
    xb = [A.take((128, D), BF16) for _ in range(2)]
    junk = A.take((128, D), BF16)

    def norm_tile(src_rows, i, dst_ap, dst_key, gvec, tag, bufs=None):
        s = i % 2
        kxs, kxb = "xs%d" % s, "xb%d" % s
        kps = ["ps%d" % (2 * s), "ps%d" % (2 * s + 1)]
        if bufs is None:
            xs_s, xb_s, junk_s = xs[s], xb[s], junk
        else:
            xs_s, xb_s, junk_s = bufs
        ss = small[:, 2 * s:2 * s + 1]
        rs = small[:, 2 * s + 1:2 * s + 2]
        kss = "ss%d" % s
        dma("sp", xs_s, src_rows, kxs, [], [kxs])
        act(junk_s, xs_s, AF.Square, [kxs], ["junk", kss], accum=ss)
        act(ss, ss, AF.Sqrt, [kss], [kss], bias=EPS, scale=1.0 / D)
        P.op("dve", lambda e: e.reciprocal(out=rs, in_=ss), [kss], [kss + "r"])
        ts("dve", xb_s[:, 0:D // 2], xs_s[:, 0:D // 2], rs, ALU.mult, [kxs, kss + "r"], [kxb + "a"])
        act(xb_s[:, D // 2:D], xs_s[:, D // 2:D], AF.Copy, [kxs, kss + "r"], [kxb + "b"], scale=rs)
        pview = bank_bf(2 * s, 2).rearrange("p (c t) -> p c t", t=128)
        for c in range(16):
            tr(pview[:, c, :], xb_s[:, c * 128:(c + 1) * 128], [kxb + ("a" if c < 8 else "b")], kps)
        tt("dve", dst_ap, pview, gvec[:, :].unsqueeze(2).to_broadcast([128, 16, 128]), ALU.mult,
           kps, [dst_key])

    for i in range(16):
        if i < 8:
            norm_tile(xp[i * 128:(i + 1) * 128, :], i, hTp[:, :, i * 128:(i + 1) * 128], "hT%d" % i, gpre, "x")
        else:
            j = i - 8
            norm_tile(xo[j * 128:(j + 1) * 128, :], i, hTo[:, :, j * 128:(j + 1) * 128], "hT%d" % i, gpre, "x")

    if dbg:
        dump("hT_own_c3", hTo[:, 3, :], [])
        dump("hT_pre_c0", hTp[:, 0, :], [])

    if stop_after <= 1:
        return finish(nc, P, stack, out_d, None)

    ipb = [0]

    def ipbank():
        b = ipb[0] % 2
        ipb[0] += 1
        return b

    P.barrier()
    A.reset()
    gk8 = [A.take((128, 8, 256), BF16) for _ in range(2)]
    gv8 = [A.take((128, 8, 512), BF16) for _ in range(2)]
    sp8 = [A.take((128, 8, 256), BF16) for _ in range(2)]
    gqT = A.take((128, 2, TOWN), BF16)
    gkT = A.take((128, 2, TOWN), BF16)
    ggs = A.take((128, 8, 512), BF16)
    S32 = A.take((128, 2, 512), F32)
    Sbf = A.take((128, 2, 512), BF16)
    ue = A.take((128, 256), F32)
    edb = [A.take((128, 256), F32) for _ in range(2)]
    kdb = [A.take((128, 256), BF16) for _ in range(2)]
    ebTb = [A.take((128, 2, 128), F32) for _ in range(2)]
    ekTb = [A.take((128, 2, 128), F32) for _ in range(2)]
    qgTb = [A.take((128, 2, 128), BF16) for _ in range(2)]
    kgTb = [A.take((128, 2, 128), BF16) for _ in range(2)]
    attn = A.take((128, 128), BF16)
    o_n = A.take((128, 512), F32)
    o_a = A.take((128, 512), BF16)
    gaT = A.take((17, TPRE + TOWN), BF16)
    P.op("pool", lambda e: e.memset(gaT, 1.0), [], ["gaT"])
    ssq = small[:, 8:9]
    rq = small[:, 9:10]

    for tb in range(4):
        hap, hk = hT(tb * 512, 512)
        b = ipbank()
        for c in range(16):
            mm(bank(b)[0:16, :], wga[:, c, :], hap[:, c, :], c == 0, c == 15, hk, ["ps%d" % b])
        cp("dve", gaT[0:16, tb * 512:(tb + 1) * 512], bank(b)[0:16, :], ["ps%d" % b], ["gaT"])

    def gla_rec_stations(hd, half, j):
        own = half == 1
        bs = half
        p_ = j % 2
        ed, kd, ebT, ekT, qgT, kgT = edb[p_], kdb[p_], ebTb[p_], ekTb[p_], qgTb[p_], kgTb[p_]
        k_ed, k_kd, k_ebT, k_ekT, k_qgT, k_kgT = ("%s%d" % (n_, p_) for n_ in ("ed", "kd", "ebT", "ekT", "qgT", "kgT"))
        tok = slice(j * 128, (j + 1) * 128)
        spj = sp8[bs][:, j, :]
        ksp, kgk, kgv = "sp8_%d_%d" % (bs, j), "gk8_%d_%d" % (bs, j), "gv8_%d_%d" % (bs, j)
        ps3v = bank(2)[:, 256:512].rearrange("p (a b) -> p a b", b=128)

        def st1():
            mm(bank(2)[:, 0:256], triS[:], spj, True, True, [ksp], ["ps2"])
            for cc in range(2):
                mm(bank(2)[:, 256 + cc * 128:256 + (cc + 1) * 128], spj[:, cc * 128:(cc + 1) * 128], triI[:],
                   True, True, [ksp], ["ps2"])
            act(ed, bank(2)[:, 0:256], AF.Exp, ["ps2"], [k_ed], scale=-1.0 / 16)
            act(ebT, ps3v, AF.Exp, ["ps2"], [k_ebT], scale=-1.0 / 16)
            if own:
                act(ekT, ps3v, AF.Exp, ["ps2"], [k_ekT], scale=1.0 / 16)
            tt("dve", kd, gk8[bs][:, j, :], ed, ALU.mult, [kgk, k_ed], [k_kd])
            if own:
                stt(qgT, gqT[:, :, tok], 1.0 / 16, ebT, ALU.mult, ALU.mult, ["gqT", k_ebT], [k_qgT])
                tt("dve", kgT, gkT[:, :, tok], ekT, ALU.mult, ["gkT", k_ekT], [k_kgT])

        def st2():
            if own:
                for cc in range(2):
                    mm(bank(4)[:, 0:128], kgT[:, cc, :], qgT[:, cc, :], cc == 0, cc == 1, [k_kgT, k_qgT], ["ps4"])
            for cc in range(2):
                mm(bank(6 + cc), kd[:, cc * 128:(cc + 1) * 128], gv8[bs][:, j, :], True, True, [k_kd, kgv],
                   ["ps%d" % (6 + cc)])
            if own:
                tt("dve", attn, bank(4)[:, 0:128], maskI[:], ALU.mult, ["ps4"], ["attn"])

        def st3():
            if own:
                mm(bank(5), attn, gv8[bs][:, j, :], True, False, ["attn", kgv], ["ps5"])
                for cc in range(2):
                    mm(bank(5), qgT[:, cc, :], Sbf[:, cc, :], False, cc == 1, [k_qgT, "Sbf%d" % cc], ["ps5"])
                act(o_a, bank(5), AF.Square, ["ps5"], ["o_a", "ssq"], accum=ssq)
                act(ssq, ssq, AF.Ln, ["ssq"], ["ssq"], bias=EPS, scale=1.0 / 512)
                act(rq, ssq, AF.Exp, ["ssq"], ["rq"], scale=-0.5)
                stt(o_n, bank(5), rq, ghead[:], ALU.mult, ALU.mult, ["ps5", "rq"], ["o_n"])

        def st4():
            for cc in range(2):
                stt(S32[:, cc, :], S32[:, cc, :], ebT[:, cc, 127:128], bank(6 + cc), ALU.mult, ALU.add,
                    ["S32_%d" % cc, k_ebT, "ps%d" % (6 + cc)], ["S32_%d" % cc])
                cp("act", Sbf[:, cc, :], S32[:, cc, :], ["S32_%d" % cc], ["Sbf%d" % cc])
            if own:
                tt("dve", o_a, o_n, ggs[:, j, :], ALU.mult, ["o_n", "ggs"], ["o_a"])

        def st5():
            if own:
                pv = bank_bf(3)[:, 0:512].rearrange("p (a b) -> p a b", b=128)
                for dvc in range(4):
                    tr(pv[:, dvc, :], o_a[:, dvc * 128:(dvc + 1) * 128], ["o_a"], ["ps3"])
                cp("dve", oT[:, hd * 4:(hd + 1) * 4, tok], pv, ["ps3"], ["oT"])

        return [st1, st2, st3, st4, st5]

    def gla_ip_chunks(hd, half, j, W):
        bs = half
        g0 = half * 1024 + j * 128
        hap, hk = hT(g0, 128)
        (wk, kwk), (wv0, kwv0), (wv1, kwv1) = W
        st = {}

        def c1a():
            st["bk"] = ipbank()
            b = st["bk"]
            for c in range(8):
                mm(bank(b)[:, 0:256], hap[:, c, :], wk[:, c, :], c == 0, c == 15, hk + [kwk], ["ps%d" % b])

        def c1b():
            b = st["bk"]
            for c in range(8, 16):
                mm(bank(b)[:, 0:256], hap[:, c, :], wk[:, c, :], c == 0, c == 15, hk + [kwk], ["ps%d" % b])
            cp("act", gk8[bs][:, j, :], bank(b)[:, 0:256], ["ps%d" % b], ["gk8_%d_%d" % (bs, j)])

        def c2a():
            st["b"] = ipbank()
            b = st["b"]
            for c in range(8):
                mm(bank(b)[:, 0:256], hap[:, c, :], wv0[:, c, :], c == 0, c == 15, hk + [kwv0], ["ps%d" % b])

        def c2b():
            b = st["b"]
            for c in range(8, 16):
                mm(bank(b)[:, 0:256], hap[:, c, :], wv0[:, c, :], c == 0, c == 15, hk + [kwv0], ["ps%d" % b])

        def c3a():
            b = st["b"]
            for c in range(8):
                mm(bank(b)[:, 256:512], hap[:, c, :], wv1[:, c, :], c == 0, c == 15, hk + [kwv1], ["ps%d" % b])

        def c3b():
            b = st["b"]
            for c in range(8, 16):
                mm(bank(b)[:, 256:512], hap[:, c, :], wv1[:, c, :], c == 0, c == 15, hk + [kwv1], ["ps%d" % b])
            cp("dve", gv8[bs][:, j, :], bank(b), ["ps%d" % b], ["gv8_%d_%d" % (bs, j)])

        def c1t():
            b = ipbank()
            pvt = bank_bf(b)[:, 0:256].rearrange("p (a b) -> p a b", b=128)
            for cc in range(2):
                tr(pvt[:, cc, :], gk8[bs][:, j, cc * 128:(cc + 1) * 128], ["gk8_%d_%d" % (bs, j)], ["ps%d" % b])
            cp("dve", gkT[:, :, j * 128:(j + 1) * 128], pvt, ["ps%d" % b], ["gkT"])

        def c4():
            b = ipbank()
            mm(bank(b)[:, 0:256], gaT[0:17, g0:g0 + 128], w2aug[0:17, hd * 256:(hd + 1) * 256], True, True,
               ["gaT"], ["ps%d" % b])
            act(ue, bank(b)[:, 0:256], AF.Exp, ["ps%d" % b], ["ue"], scale=-1.0)
            act(sp8[bs][:, j, :], ue, AF.Ln, ["ue"], ["sp8_%d_%d" % (bs, j)], bias=1.0)

        if half == 1:
            return [c1a, c1b, c2a, c2b, c1t, c3a, c3b, c4]
        return [c1a, c1b, c2a, c2b, c3a, c3b, c4]

    def gla_load_kv(hd):
        return (load_w(w_in, C_GK + hd * 256, slot=0), load_w(w_in, C_GV + hd * 512, slot=1),
                load_w(w_in, C_GV + hd * 512 + 256, slot=2))

    def reorder(S):
        out = [S[0][0]]
        for j in range(len(S)):
            out.append(S[j][1])
            if j + 1 < len(S):
                out.append(S[j + 1][0])
            out += S[j][2:]
        return out

    def interleave(rec_list, fill_list, skip):
        gaps = len(rec_list)
        nfill = len(fill_list)
        fi = 0
        for gi, st in enumerate(rec_list):
            st()
            if gi < skip or nfill == 0:
                continue
            remaining_gaps = gaps - gi
            n = -(-(nfill - fi) // remaining_gaps)
            for _ in range(n):
                if fi < nfill:
                    fill_list[fi]()
                    fi += 1
        while fi < nfill:
            fill_list[fi]()
            fi += 1

    Wkv = gla_load_kv(0)
    wq, kwq = load_w(w_in, C_GQ, slot=3)
    for ch in [c for j in range(8) for c in gla_ip_chunks(0, 0, j, Wkv)]:
        ch()
    for hd in range(4):
        wk, kwk = Wkv[0]
        for (wsrc, kw, dst, kdst) in ((wq, kwq, gqT, "gqT"),):
            for cc in range(2):
                for tb in range(2):
                    hap, hk = hT(TPRE + tb * 512, 512)
                    b = ipbank()
                    for c in range(16):
                        mm(bank(b), wsrc[:, c, cc * 128:(cc + 1) * 128], hap[:, c, :], c == 0, c == 15,
                           hk + [kw], ["ps%d" % b])
                    cp("act", dst[:, cc, tb * 512:(tb + 1) * 512], bank(b), ["ps%d" % b], [kdst])
        for gh in range(2):
            wg, kwg = load_w(w_in, C_GG + hd * 512 + gh * 256, slot=3)
            for j in range(8):
                hap, hk = hT(TPRE + j * 128, 128)
                b = ipbank()
                for c in range(16):
                    mm(bank(b)[:, 0:256], hap[:, c, :], wg[:, c, :], c == 0, c == 15, hk + [kwg], ["ps%d" % b])
                act(ggs[:, j, gh * 256:(gh + 1) * 256], bank(b)[:, 0:256], AF.Silu, ["ps%d" % b], ["ggs"])
        if hd + 1 < 4:
            wq, kwq = load_w(w_in, C_GQ + (hd + 1) * 256, slot=3)
        P.op("pool", lambda e: e.memset(S32[:], 0.0), [], ["S32_0", "S32_1"])
        P.op("pool", lambda e: e.memset(Sbf[:], 0.0), [], ["Sbf0", "Sbf1"])
        rec = reorder([gla_rec_stations(hd, 0, j) for j in range(8)])
        fill = [c for j in range(8) for c in gla_ip_chunks(hd, 1, j, Wkv)]
        interleave(rec, fill, 0)
        rec = reorder([gla_rec_stations(hd, 1, j) for j in range(8)])
        if hd + 1 < 4:
            Wkv = gla_load_kv(hd + 1)
            fill = [c for j in range(8) for c in gla_ip_chunks(hd + 1, 0, j, Wkv)]
            interleave(rec, fill, 12)
        else:
            interleave(rec, [], 0)


    if dbg:
        dump("oT_a0", oT[:, 0, :], [])
        dump("oT_a9", oT[:, 9, :], [])
    if stop_after <= 2:
        return finish(nc, P, stack, out_d, None)

    P.barrier()
    A.reset()
    mergedT = A.take((128, 16, TOWN), BF16)
    MERGED_END = A.off
    opc = [0]

    def outproj_setup():
        A.reset(MERGED_END)
        sg_ = [A.take((128, 512), F32) for _ in range(2)]
        pt_ = [A.take((128, 512), F32) for _ in range(2)]
        xs_ = [A.take((128, 16, 256), BF16) for _ in range(3)]
        ring = [(wsl[:, i], "wsl%d" % i) for i in range(4)] + [(xs_[i], "xw%d" % i) for i in range(3)]
        return sg_, pt_, ring

    def outproj(wd, gate_col0, first, setup):
        sgm, ptmp, ring = setup
        rr = [0]

        def ld(src_d, col0):
            ap, key = ring[rr[0] % len(ring)]
            rr[0] += 1
            dma("pool", ap[:, :, 0:256], src_d[:, col0:col0 + 256].rearrange("(c p) n -> p c n", p=128),
                key, [], [key])
            return ap, key

        loaded = {}

        def ensure(i):
            if i < 8 and i not in loaded:
                loaded[i] = (ld(wd, i * 256), ld(w_in, gate_col0 + i * 256))

        ensure(0)
        ensure(1)
        for cb2 in range(8):
            ensure(cb2 + 2)
            (wp, kwp), (wg, kwg) = loaded[cb2]
            for sub in range(2):
                cb = cb2 * 2 + sub
                for tb in range(2):
                    i = opc[0] % 2
                    opc[0] += 1
                    pb, gb = i, 2 + i
                    tsl = slice(tb * 512, (tb + 1) * 512)
                    for k in range(16):
                        mm(bank(pb), wp[:, k, sub * 128:(sub + 1) * 128], oT[:, k, tsl], k == 0, k == 15,
                           ["oT", kwp], ["ps%d" % pb])
                    hap, hk = hT(TPRE + tb * 512, 512)
                    for c in range(16):
                        mm(bank(gb), wg[:, c, sub * 128:(sub + 1) * 128], hap[:, c, :], c == 0, c == 15,
                           hk + [kwg], ["ps%d" % gb])
                    act(sgm[i], bank(gb), AF.Sigmoid, ["ps%d" % gb], ["sgm%d" % i])
                    if first:
                        tt("dve", mergedT[:, cb, tsl], bank(pb), sgm[i], ALU.mult, ["ps%d" % pb, "sgm%d" % i],
                           ["mT%d" % cb])
                    else:
                        tt("dve", ptmp[i], bank(pb), sgm[i], ALU.mult, ["ps%d" % pb, "sgm%d" % i], ["ptmp%d" % i])
                        tt("dve", mergedT[:, cb, tsl], mergedT[:, cb, tsl], ptmp[i], ALU.add,
                           ["mT%d" % cb, "ptmp%d" % i], ["mT%d" % cb])

    outproj(w_pa, C_GATE, True, outproj_setup())
    if dbg:
        dump("mT_a0", mergedT[:, 0, :], [])
        dump("mT_a7", mergedT[:, 7, :], [])
    if stop_after <= 3:
        return finish(nc, P, stack, out_d, None)

    P.barrier()
    A.reset(MERGED_END)
    skT = [A.take((128, TPRE + TOWN), BF16) for _ in range(2)]
    sqT = [A.take((128, TOWN), BF16) for _ in range(2)]
    sgs = [A.take((128, TOWN), BF16) for _ in range(2)]
    sv2 = A.take((128, 16, 256), BF16)
    Lsum = [A.take((128, 512), BF16) for _ in range(2)]
    Eb = [A.take((128, 512), F32) for _ in range(4)]
    SPb = [A.take((128, 512), BF16) for _ in range(2)]
    Wtb = [A.take((128, 512), BF16) for _ in range(2)]
    sil = A.take((128, 512), F32)
    SCALE = 1.0 / math.sqrt(128.0)
    gstep = [0]
    ggrp = [0]
    ZB = (2, 3, 4)
    AB = (5, 6)
    OB = (7, 0)

    def sb_ipbank():
        b = 2 + ipb[0] % 2
        ipb[0] += 1
        return b

    SUBK = 4

    def sb_ip_chunks(hh, par, W, bankfn):
        (wsq, kwsq), (wsk, kwsk), (wsg, kwsg) = W
        hs = slice(0, 128)
        chunks = []

        def mk(wt, kw, g0, kind, dst, kdst):
            st = {}
            subs = []
            for q in range(16 // SUBK):
                def f(q=q):
                    hap, hk = hT(g0, 512)
                    if q == 0:
                        st["b"] = bankfn()
                    b = st["b"]
                    for c in range(q * SUBK, (q + 1) * SUBK):
                        mm(bank(b), wt[:, c, hs], hap[:, c, :], c == 0, c == 15, hk + [kw], ["ps%d" % b])
                    if q == 16 // SUBK - 1:
                        if kind == "act":
                            cp("act", dst, bank(b), ["ps%d" % b], [kdst])
                        elif kind == "dve":
                            cp("dve", dst, bank(b), ["ps%d" % b], [kdst])
                        else:
                            act(sil, bank(b), AF.Exp, ["ps%d" % b], ["sil"], scale=-1.0)
                            act(sil, sil, AF.Ln, ["sil"], ["sil"], bias=1.0)
                            act(sil, sil, AF.Exp, ["sil"], ["sil"], scale=-1.0)
                            tt("dve", dst, bank(b), sil, ALU.mult, ["ps%d" % b, "sil"], [kdst])
                subs.append(f)
            return subs

        for tb in range(4):
            chunks += mk(wsk, kwsk, tb * 512, "act" if tb % 2 == 0 else "dve",
                         skT[par][:, tb * 512:(tb + 1) * 512], "skT%d" % par)
        for tb in range(2):
            chunks += mk(wsq, kwsq, TPRE + tb * 512, "dve", sqT[par][:, tb * 512:(tb + 1) * 512],
                         "sqT%d" % par)
            chunks += mk(wsg, kwsg, TPRE + tb * 512, "silu", sgs[par][:, tb * 512:(tb + 1) * 512],
                         "sgs%d" % par)
        return chunks

    def sb_pair(hp, fillA, fillB, load_at_start, load_mid):
        steps = []
        for hh in range(2):
            for qg in range(2):
                nkb = 12 + 4 * qg
                g = ggrp[0]
                ggrp[0] += 1
                for gi, kb in enumerate(reversed(range(nkb))):
                    r0 = max(0, kb - 8 - 4 * qg)
                    n = gstep[0]
                    gstep[0] += 1
                    steps.append(dict(n=n, qg=qg, kb=kb, gi=gi, g=g, r0=r0, diag=kb >= 8 + 4 * qg,
                                      first=gi == 0, last=kb == 0, hh=hh, par=hh, head=hp * 2 + hh))

        def stage(st, sg):
            n, r0, gi, par, hh = st["n"], st["r0"], st["gi"], st["par"], st["hh"]
            kskT, ksqT, ksgs = "skT%d" % par, "sqT%d" % par, "sgs%d" % par
            zb, ab, ob = ZB[n % 3], AB[n % 2], OB[st["g"] % 2]
            E, kE = Eb[n % 4], "E%d" % (n % 4)
            SP, kSP = SPb[n % 2], "SPb%d" % (n % 2)
            Wt, kW = Wtb[n % 2], "Wt%d" % (n % 2)
            cols = slice(r0 * 128, 512)
            dcols = slice(r0 * 128, (r0 + 1) * 128)
            rcols = slice((r0 + 1) * 128, 512)
            Lold, kLold = Lsum[gi % 2], "Ls%d" % (gi % 2)
            Lnew, kLnew = Lsum[(gi + 1) % 2], "Ls%d" % ((gi + 1) % 2)
            lc = rcols if st["diag"] else cols
            has_l = (not st["first"]) and (not st["diag"] or r0 < 3)
            kz, ka, ko = "ps%d" % zb, "ps%d" % ab, "ps%d" % ob
            qg, kb = st["qg"], st["kb"]
            if sg == 0:
                mm(bank(zb)[:, cols], skT[par][:, kb * 128:(kb + 1) * 128],
                   sqT[par][:, qg * 512 + r0 * 128:(qg + 1) * 512], True, True, [kskT, ksqT], [kz])
            elif sg == 1:
                act(E[:, cols], bank(zb)[:, cols], AF.Exp, [kz], [kE], scale=SCALE)
                act(E[:, cols], E[:, cols], AF.Ln, [kE], [kE], bias=1.0)
            elif sg == 2:
                if st["diag"]:
                    tt("dve", SP[:, dcols], E[:, dcols], maskS[:], ALU.mult, [kE], [kSP])
                    if r0 < 3:
                        cp("dve", SP[:, rcols], E[:, rcols], [kE], [kSP])
                else:
                    cp("dve", SP[:, cols], E[:, cols], [kE], [kSP])
                stt(E[:, cols], bank(zb)[:, cols], SCALE, E[:, cols], ALU.mult, ALU.subtract, [kz, kE], [kE])
                if st["diag"]:
                    cp("pool", Lnew[:, dcols], SP[:, dcols], [kSP], [kLnew])
                if has_l:
                    tt("pool", Lnew[:, lc], Lold[:, lc], SP[:, lc], ALU.add, [kLold, kSP], [kLnew])
            elif sg == 3:
                mm(bank(ab)[:, cols], triS[:], SP[:, cols], True, not has_l, [kSP], [ka])
                if has_l:
                    mm(bank(ab)[:, lc], ones[:], Lold[:, lc], False, True, [kLold], [ka])
            elif sg == 4:
                tt("dve", E[:, cols], E[:, cols], bank(ab)[:, cols], ALU.subtract, [kE, ka], [kE])
            elif sg == 5:
                act(Wt[:, cols], E[:, cols], AF.Exp, [kE], [kW])
                if st["diag"]:
                    tt("pool", Wt[:, dcols], Wt[:, dcols], maskS[:], ALU.mult, [kW], [kW])
            elif sg == 6:
                mm(bank(ob)[:, cols], sv2[:, kb, hh * 128:(hh + 1) * 128], Wt[:, cols], st["first"], st["last"],
                   ["sv2", kW], [ko], sgc=True)
                if st["last"]:
                    tt("dve", oT[:, st["head"], qg * 512:(qg + 1) * 512], bank(ob),
                       sgs[par][:, qg * 512:(qg + 1) * 512], ALU.mult, [ko, ksgs], ["oT"])

        NST = 7
        fa = 0
        fb = 0
        if load_at_start is not None:
            load_at_start()
        for slot in range(len(steps) + NST - 1):
            for sg in reversed(range(NST)):
                i = slot - sg
                if 0 <= i < len(steps):
                    stage(steps[i], sg)
            if fa < len(fillA) and slot <= 31:
                nq = -(-(len(fillA) - fa) // (32 - slot))
                for _ in range(nq):
                    fillA[fa]()
                    fa += 1
            if slot == 32:
                assert fa == len(fillA)
                if load_mid is not None:
                    load_mid()
            if slot >= 32 and fb < len(fillB):
                nq = -(-(len(fillB) - fb) // max(1, 61 - slot))
                for _ in range(nq):
                    if fb < len(fillB):
                        fillB[fb]()
                        fb += 1
        assert fa == len(fillA)
        while fb < len(fillB):
            fillB[fb]()
            fb += 1

    HS = [(wsl[:, s_, :, h_ * 128:(h_ + 1) * 128], "wh%d_%d" % (s_, h_)) for s_ in (0, 1, 3) for h_ in (0, 1)]
    grpA = [HS[0], HS[2], HS[4]]
    grpB = [HS[1], HS[3], HS[5]]
    first_load = [True]

    def ld_half(slotkey, col0):
        ap, key = slotkey
        nf = not first_load[0]
        first_load[0] = False
        dma("pool", ap, w_in[:, col0:col0 + 128].rearrange("(c p) n -> p c n", p=128), key, [], [key], nofence=nf)
        return ap, key

    def sb_load_head(hp, hh, grp):
        c = hp * 256 + hh * 128
        return (ld_half(grp[0], C_SQ + c), ld_half(grp[1], C_SK + c), ld_half(grp[2], C_SG + c))

    def sb_load_sv(hp):
        first_load[0] = False
        return load_w(w_in, C_SV + hp * 256, slot=2)

    WB = sb_load_head(0, 0, grpB)
    wsv_cur = sb_load_sv(0)
    WA = sb_load_head(0, 1, grpA)
    for ch in sb_ip_chunks(0, 0, WB, sb_ipbank):
        ch()
    for hp in range(8):
        wsv, kwsv = wsv_cur
        for i in range(16):
            hap, hk = hT(i * 128, 128)
            b = sb_ipbank()
            for c in range(16):
                mm(bank(b)[:, 0:256], hap[:, c, :], wsv[:, c, :], c == 0, c == 15, hk + [kwsv], ["ps%d" % b])
            cp("act" if i % 2 == 0 else "dve", sv2[:, i, :], bank(b)[:, 0:256], ["ps%d" % b], ["sv2"])
        fillA = sb_ip_chunks(1, 1, WA, lambda: 1)
        if hp + 1 < 8:
            st_ = {}

            def load_start(hp=hp, st_=st_):
                st_["sv"] = sb_load_sv(hp + 1)
                st_["WB"] = sb_load_head(hp + 1, 0, grpB)

            def load_mid(hp=hp, st_=st_):
                st_["WA"] = sb_load_head(hp + 1, 1, grpA)

            class _Lazy(list):
                pass
            load_start()
            fillB = sb_ip_chunks(0, 0, st_["WB"], lambda: 1)
            sb_pair(hp, fillA, fillB, None, load_mid)
            wsv_cur, WA = st_["sv"], st_["WA"]
        else:
            sb_pair(hp, fillA, [], None, None)

    if dbg:
        dump("oT_b0", oT[:, 0, :], [])
        dump("oT_b13", oT[:, 13, :], [])
    if stop_after <= 4:
        return finish(nc, P, stack, out_d, None)

    P.barrier()
    outproj(w_pb, C_GATE + 2048, False, outproj_setup())
    if dbg:
        dump("mT_ab5", mergedT[:, 5, :], [])
    if stop_after <= 5:
        return finish(nc, P, stack, out_d, None)

    P.barrier()
    A.reset(MERGED_END)
    A2.reset()
    mxs = A.take((128, D), F32)
    mxb = A.take((128, D), BF16)
    mjunk = A.take((128, D), BF16)
    sce = A.take((128, 256), F32)
    Pn = A.take((128, 256), BF16)
    pT = A.take((128, 2, 128), BF16)
    mhT = A2.take((128, 16, 256), BF16)
    mkT = A2.take((128, 16, 256), BF16)
    mv = A2.take((128, 2, D), BF16)
    mqT = A2.take((128, 4, TOWN), BF16)
    for i in range(2):
        norm_tile(memb[i * 128:(i + 1) * 128, :], 0, mhT[:, :, i * 128:(i + 1) * 128], "mhT", gmem, "m",
                  bufs=(mxs, mxb, mjunk))
    for cb2 in range(8):
        wmk, kw = load_w(w_mem_kv, cb2 * 256)
        for sub in range(2):
            b = ipbank()
            for c in range(16):
                mm(bank(b)[:, 0:256], wmk[:, c, sub * 128:(sub + 1) * 128], mhT[:, c, :], c == 0, c == 15,
                   ["mhT", kw], ["ps%d" % b])
            cp("act", mkT[:, cb2 * 2 + sub, :], bank(b)[:, 0:256], ["ps%d" % b], ["mkT"])
    for cb2 in range(8):
        wmv, kw = load_w(w_mem_kv, 2048 + cb2 * 256)
        for mt in range(2):
            b = ipbank()
            for c in range(16):
                mm(bank(b)[:, 0:256], mhT[:, c, mt * 128:(mt + 1) * 128], wmv[:, c, :], c == 0, c == 15,
                   ["mhT", kw], ["ps%d" % b])
            cp("dve", mv[:, mt, cb2 * 256:(cb2 + 1) * 256], bank(b)[:, 0:256], ["ps%d" % b], ["mv"])
    MSCALE = 1.0 / math.sqrt(512.0)
    mqT2 = [mqT, A.take((128, 4, TOWN), BF16)]
    sceb = [sce, A.take((128, 256), F32)]
    Pnb = [Pn, A.take((128, 256), BF16)]
    pTb = [pT, A.take((128, 2, 128), BF16)]

    def mq_chunks(hd, W2):
        par = hd % 2
        subs = []
        for half in range(2):
            wmq, kw = W2[half]
            for sub in range(2):
                cc = half * 2 + sub
                for tb in range(2):
                    st = {}
                    for q in range(4):
                        def f(q=q, st=st, wmq=wmq, kw=kw, sub=sub, cc=cc, tb=tb):
                            hap, hk = hT(TPRE + tb * 512, 512)
                            if q == 0:
                                st["b"] = ipbank()
                            b = st["b"]
                            for c in range(q * 4, (q + 1) * 4):
                                mm(bank(b), wmq[:, c, sub * 128:(sub + 1) * 128], hap[:, c, :], c == 0, c == 15,
                                   hk + [kw], ["ps%d" % b])
                            if q == 3:
                                cp("act", mqT2[par][:, cc, tb * 512:(tb + 1) * 512], bank(b), ["ps%d" % b],
                                   ["mqT%d" % par])
                        subs.append(f)
        return subs

    def mq_load(hd):
        s0 = (hd % 2) * 2
        return [load_w(w_in, C_MQ + hd * 512, slot=s0), load_w(w_in, C_MQ + hd * 512 + 256, slot=s0 + 1)]

    its = [(hd, j) for hd in range(4) for j in range(8)]

    def mstage(n, sg):
        hd, j = its[n]
        par = hd % 2
        i2 = n % 2
        tok = slice(j * 128, (j + 1) * 128)
        sb_, tb_, ob_ = 2 + i2, 4 + i2, 6 + i2
        ks, kt, ko = "ps%d" % sb_, "ps%d" % tb_, "ps%d" % ob_
        mx, nb, sm, rs = (small[:, 16 + 4 * i2 + q:17 + 4 * i2 + q] for q in range(4))
        kq = "msc%d" % i2
        sce_, Pn_, pT_ = sceb[i2], Pnb[i2], pTb[i2]
        ptv = bank_bf(tb_)[:, 0:256].rearrange("p (a b) -> p a b", b=128)
        if sg == 0:
            for cc in range(4):
                mm(bank(sb_)[:, 0:256], mqT2[par][:, cc, tok], mkT[:, hd * 4 + cc, :], cc == 0, cc == 3,
                   ["mqT%d" % par, "mkT"], [ks])
        elif sg == 1:
            P.op("dve", lambda e: e.tensor_reduce(out=mx, in_=bank(sb_)[:, 0:256], axis=AX.X, op=ALU.max),
                 [ks], [kq + "mx"])
            ts("dve", nb, mx, -MSCALE, ALU.mult, [kq + "mx"], [kq + "nb"])
            act(sce_, bank(sb_)[:, 0:256], AF.Exp, [ks, kq + "nb"], ["sce%d" % i2, kq + "sm"], bias=nb,
                scale=MSCALE, accum=sm)
            P.op("dve", lambda e: e.reciprocal(out=rs, in_=sm), [kq + "sm"], [kq + "rs"])
            ts("dve", Pn_, sce_, rs, ALU.mult, ["sce%d" % i2, kq + "rs"], ["Pn%d" % i2])
        elif sg == 2:
            for mt in range(2):
                tr(ptv[:, mt, :], Pn_[:, mt * 128:(mt + 1) * 128], ["Pn%d" % i2], [kt])
            cp("act", pT_, ptv, [kt], ["pT%d" % i2])
        elif sg == 3:
            for dvc in range(4):
                for mt in range(2):
                    mm(bank(ob_)[:, dvc * 128:(dvc + 1) * 128],
                       mv[:, mt, hd * 512 + dvc * 128:hd * 512 + (dvc + 1) * 128], pT_[:, mt, :], mt == 0, mt == 1,
                       ["mv", "pT%d" % i2], [ko])
            cp("dve", oT[:, hd * 4:(hd + 1) * 4, tok], bank(ob_).rearrange("p (a b) -> p a b", b=128), [ko], ["oT"])

    Wq = mq_load(0)
    for f in mq_chunks(0, Wq):
        f()
    Wq_next = mq_load(1)
    fillers = []
    NSTM = 4
    for slot in range(len(its) + NSTM - 1):
        if slot < len(its) and slot % 8 == 0:
            hd = slot // 8
            if hd + 1 < 4:
                fillers = mq_chunks(hd + 1, Wq_next)
                if hd + 2 < 4:
                    Wq_next = mq_load(hd + 2)
            else:
                fillers = []
            fpos = 0
        for sg in reversed(range(NSTM)):
            n = slot - sg
            if 0 <= n < len(its):
                mstage(n, sg)
        if slot < len(its):
            left = 8 - slot % 8
            nq = -(-(len(fillers) - fpos) // left)
            for _ in range(nq):
                if fpos < len(fillers):
                    fillers[fpos]()
                    fpos += 1
    if dbg:
        dump("oT_m2", oT[:, 2, :], [])
        dump("oT_m15", oT[:, 15, :], [])
    if stop_after <= 6:
        return finish(nc, P, stack, out_d, None)

    P.barrier()
    hTo_flat = hTo[:].rearrange("p a b -> p (a b)")
    wo_lo = hTp_flat.rearrange("p (k n) -> p k n", n=D)
    wo_hi = hTo_flat.rearrange("p (k n) -> p k n", n=D)
    for k in range(8):
        dma("pool", wo_lo[:, k, :], w_out[k * 128:(k + 1) * 128, :], "wo%d" % k, [],
            ["wo%d" % k, "mhT", "mkT", "mv", "mqT"], max_dma_last_dim=4096)
    outproj(w_pm, C_GATE + 4096, False, outproj_setup())
    if dbg:
        dump("mT_f5", mergedT[:, 5, :], [])
    if stop_after <= 7:
        return finish(nc, P, stack, out_d, None)

    P.barrier()
    A.reset(MERGED_END)
    xs2 = [A.take((128, D), F32) for _ in range(2)]
    gpost = A.take((128, D), F32)
    oT_flat = oT[:].rearrange("p a b -> p (a b)")
    A3 = Arena(oT_flat, 32 * 1024)
    yo = [A3.take((128, D), F32) for _ in range(2)]
    dma("sp", gpost, g_post_d.partition_broadcast(128), "c6", [], ["gpost"])
    for k in range(8, 16):
        dma("pool", wo_hi[:, k - 8, :], w_out[k * 128:(k + 1) * 128, :], "wo%d" % k, [], ["wo%d" % k],
            max_dma_last_dim=4096)
    ss8, rs8 = small[:, 24:25], small[:, 25:26]
    for j in range(8):
        s = j % 2
        tok = slice(j * 128, (j + 1) * 128)
        dma("sp", xs2[s], xo[tok, :], "xs2_%d" % s, [], ["xs2_%d" % s])
        for cb in range(4):
            b = 4 * s + cb
            for k in range(16):
                wk_ = wo_lo[:, k, cb * 512:(cb + 1) * 512] if k < 8 else wo_hi[:, k - 8, cb * 512:(cb + 1) * 512]
                mm(bank(b), mergedT[:, k, tok], wk_, k == 0, k == 15, ["mT%d" % k, "wo%d" % k], ["ps%d" % b])
        yv = ps[:, 4 * s:4 * s + 4, :].rearrange("p b n -> p (b n)")
        kpy = ["ps%d" % (4 * s + q) for q in range(4)]
        act(yo[s], yv, AF.Square, kpy, ["yo%d" % s, "ss8"], accum=ss8)
        act(ss8, ss8, AF.Sqrt, ["ss8"], ["ss8"], bias=EPS, scale=1.0 / D)
        P.op("dve", lambda e: e.reciprocal(out=rs8, in_=ss8), ["ss8"], ["rs8"])
        stt(yo[s], yv, rs8, gpost, ALU.mult, ALU.mult, kpy + ["rs8", "gpost"], ["yo%d" % s])
        tt("pool", yo[s], yo[s], xs2[s], ALU.add, ["yo%d" % s, "xs2_%d" % s], ["yo%d" % s])
        dma("sp", out_d[tok, :], yo[s], "out%d" % s, ["yo%d" % s], ["outd%d" % s])
    P.op("sp", lambda e: None, ["outd0", "outd1"], [])
    return finish(nc, P, stack, out_d, None)


def finish(nc, P, stack, out_d, last):
    with stack:
        P.emit(nc, stack)
    return nc, P


def _prep_inputs(inputs):
    x = np.ascontiguousarray(inputs["x"], dtype=np.float32)
    mem = np.ascontiguousarray(inputs["mem"], dtype=np.float32)
    shared = {k: np.ascontiguousarray(inputs[k], dtype=np.float32) for k in (
        "norm_pre_g", "norm_post_g", "norm_mem_g", "w_in", "gla_a_w2", "gla_a_b", "gla_head_norm_g",
        "w_mem_kv", "w_proj_gla", "w_proj_sb", "w_proj_mem", "w_out")}
    zeros = np.zeros((TPRE, D), np.float32)
    in_maps = []
    for core in range(8):
        b, hf = core // 2, core % 2
        m = dict(shared)
        m["xo"] = np.ascontiguousarray(x[b, hf * 1024:(hf + 1) * 1024])
        m["xp"] = np.ascontiguousarray(x[b, 0:1024]) if hf == 1 else zeros
        m["memb"] = mem[b]
        in_maps.append(m)
    return in_maps


def kernel(**inputs):
    in_maps = _prep_inputs(inputs)
    nc, P = build_program()
    res = run_bass_kernel_spmd(nc, in_maps, core_ids=list(range(8)))
    out = np.empty((4, 2048, D), np.float32)
    for core in range(8):
        b, hf = core // 2, core % 2
        out[b, hf * 1024:(hf + 1) * 1024] = res.results[core]["out"]
    return out
```

```python
import math
from contextlib import ExitStack

import numpy as np
import concourse.bass as bass
import concourse.mybir as mybir
from concourse.bass_utils import run_bass_kernel_spmd

F32 = mybir.dt.float32
BF16 = mybir.dt.bfloat16
AF = mybir.ActivationFunctionType
ALU = mybir.AluOpType
AX = mybir.AxisListType

D = 2048
TOWN = 1024
TPRE = 1024
NMEM = 256
EPS = 1e-6
IN_TOTAL = 22544
C_GQ, C_GK, C_GV, C_GG, C_GA = 0, 1024, 2048, 4096, 6144
C_SQ, C_SK, C_SV, C_SG = 6160, 8208, 10256, 12304
C_MQ, C_GATE = 14352, 16400

SAME_ENGINE_SYNC = True
STRICT_SAME_ENGINE = True
ENGS = ("pe", "act", "dve", "pool", "sp")


class Prog:
    def __init__(self):
        self.ops = []
        self.lastw = {}
        self.readers = {}
        self.dma_cnt = {}
        self.barrier_deps = {e: set() for e in ENGS}
        self.pe_exempt = False

    def op(self, eng, fn, reads=(), writes=(), dma_slot=None, nofence=False):
        idx = len(self.ops)
        deps = set()
        raw = set()
        for k in reads:
            w = self.lastw.get(k)
            if w is not None:
                deps.add(w)
                raw.add(w)
        for k in writes:
            w = self.lastw.get(k)
            if w is not None:
                deps.add(w)
            deps.update(self.readers.get(k, ()))
        if not nofence and not (eng == "pe" and self.pe_exempt):
            deps |= self.barrier_deps[eng]
            raw |= self.barrier_deps[eng]
            self.barrier_deps[eng] = set()
        rec = dict(eng=eng, fn=fn, deps=deps, raw=raw, dma=None, signal=False)
        if dma_slot is not None:
            c = self.dma_cnt.get(dma_slot, 0) + 1
            self.dma_cnt[dma_slot] = c
            rec["dma"] = (dma_slot, 16 * c)
        self.ops.append(rec)
        for k in reads:
            self.readers.setdefault(k, []).append(idx)
        for k in writes:
            self.lastw[k] = idx
            self.readers[k] = []
        return idx

    def barrier(self, full=False):
        self.pe_exempt = not full
        last = {}
        for i, o in enumerate(self.ops):
            if o["dma"] is not None:
                last[("dma", o["dma"][0])] = i
            else:
                last[o["eng"]] = i
        deps = set(last.values())
        for e in ENGS:
            self.barrier_deps[e] = set(deps)

    def emit(self, nc, stack):
        ops = self.ops
        for i, o in enumerate(ops):
            for d in o["deps"]:
                p = ops[d]
                if p["dma"] is not None:
                    continue
                if p["eng"] == o["eng"]:
                    if p["eng"] == "pe":
                        continue
                    if not (SAME_ENGINE_SYNC and (STRICT_SAME_ENGINE or d in o["raw"])):
                        continue
                p["signal"] = True
        cnt = {e: 0 for e in ENGS}
        for o in ops:
            if o["dma"] is None and o["signal"]:
                cnt[o["eng"]] += 1
                o["sig"] = (o["eng"], cnt[o["eng"]])
            elif o["dma"] is not None:
                o["sig"] = ("dma:" + o["dma"][0], o["dma"][1])
        sems = {}
        for e in ENGS:
            sems[e] = stack.enter_context(nc.semaphore("s_" + e))
        for slot in self.dma_cnt:
            sems["dma:" + slot] = stack.enter_context(nc.semaphore("d_" + slot))
        seen = {e: {} for e in ENGS}
        for o in ops:
            e = o["eng"]
            waits = {}
            for d in o["deps"]:
                p = ops[d]
                if p["dma"] is None:
                    if p["eng"] == e:
                        if e == "pe" or not (SAME_ENGINE_SYNC and (STRICT_SAME_ENGINE or d in o["raw"])):
                            continue
                s, v = p["sig"]
                if seen[e].get(s, 0) >= v:
                    continue
                waits[s] = max(waits.get(s, 0), v)
            for s, v in waits.items():
                seen[e][s] = v
            o["waits"] = sorted(waits.items())
        block = stack.enter_context(nc.Block())
        nwaits = sum(len(o["waits"]) for o in ops)
        self.stats = dict(n_ops=len(ops), n_waits=nwaits,
                          per_eng={e: sum(1 for o in ops if o["eng"] == e) for e in ENGS})

        def run(engname, eng):
            for o in ops:
                if o["eng"] != engname:
                    continue
                for s, v in o["waits"]:
                    eng.wait_ge(sems[s], v)
                ins = o["fn"](eng)
                if ins is None:
                    continue
                if o["dma"] is not None:
                    ins.then_inc(sems["dma:" + o["dma"][0]], 16)
                elif o["signal"]:
                    ins.then_inc(sems[engname], 1)

        @block.tensor
        def _(eng):
            run("pe", eng)

        @block.scalar
        def _(eng):
            run("act", eng)

        @block.vector
        def _(eng):
            run("dve", eng)

        @block.gpsimd
        def _(eng):
            run("pool", eng)

        @block.sync
        def _(eng):
            run("sp", eng)


def build_program(dbg=None, stop_after=99):
    nc = bass.Bass("TRN2", target_bir_lowering=False)
    P = Prog()
    stack = ExitStack()

    def din(name, shape):
        return nc.dram_tensor(name, list(shape), F32, kind="ExternalInput").ap()

    xo = din("xo", (TOWN, D))
    xp = din("xp", (TPRE, D))
    memb = din("memb", (NMEM, D))
    g_pre_d = din("norm_pre_g", (D,))
    g_post_d = din("norm_post_g", (D,))
    g_mem_d = din("norm_mem_g", (D,))
    w_in = din("w_in", (D, IN_TOTAL))
    w2_d = din("gla_a_w2", (16, 1024))
    ab_d = din("gla_a_b", (1024,))
    ghead_d = din("gla_head_norm_g", (512,))
    w_mem_kv = din("w_mem_kv", (D, 4096))
    w_pa = din("w_proj_gla", (D, D))
    w_pb = din("w_proj_sb", (D, D))
    w_pm = din("w_proj_mem", (D, D))
    w_out = din("w_out", (D, D))
    out_d = nc.dram_tensor("out", [TOWN, D], F32, kind="ExternalOutput").ap()
    dbg_d = {}
    if dbg:
        for name, shape in dbg.items():
            dbg_d[name] = nc.dram_tensor("dbg_" + name, list(shape), F32, kind="ExternalOutput").ap()

    def sb(name, shape, dt):
        return stack.enter_context(nc.sbuf_tensor(name, list(shape), dt))

    hTp = sb("hTp", (128, 16, TPRE), BF16)
    hTo = sb("hTo", (128, 16, TOWN), BF16)
    oT = sb("oT", (128, 16, TOWN), BF16)
    wsl = sb("wsl", (128, 4, 16, 256), BF16)
    ARENA_B = 72 * 1024
    arena = sb("arena", (128, ARENA_B // 2), BF16)
    ident = sb("ident", (128, 128), BF16)
    triS = sb("triS", (128, 128), BF16)
    triI = sb("triI", (128, 128), BF16)
    ones = sb("ones", (128, 128), BF16)
    maskS = sb("maskS", (128, 128), F32)
    maskI = sb("maskI", (128, 128), F32)
    gpre = sb("gpre", (128, 16), F32)
    gmem = sb("gmem", (128, 16), F32)
    ghead = sb("ghead", (128, 512), F32)
    w2aug = sb("w2aug", (17, 1024), BF16)
    wga = sb("wga", (128, 16, 16), BF16)
    small = sb("small", (128, 64), F32)
    ps = stack.enter_context(nc.psum_tensor("ps", [128, 8, 512], F32))

    def bank(b):
        return ps[:, b, :]

    def bank_bf(b, nb=1):
        return ps[:, b:b + nb, :].rearrange("p b n -> p (b n)").bitcast(BF16)

    class Arena:
        def __init__(self, base, nbytes):
            self.base, self.nbytes, self.off = base, nbytes, 0

        def reset(self, off=0):
            self.off = off

        def take(self, shape, dt):
            esz = 4 if dt == F32 else 2
            n = int(np.prod(shape[1:]))
            nb = (n * esz + 31) // 32 * 32
            assert self.off + nb <= self.nbytes, ("arena overflow", self.off, nb, self.nbytes)
            v = self.base[0:shape[0], self.off // 2:(self.off + n * esz) // 2]
            self.off += nb
            if dt == F32:
                v = v.bitcast(F32)
            if len(shape) == 3:
                v = v.rearrange("p (a b) -> p a b", b=shape[2])
            return v

    A = Arena(arena, ARENA_B)
    hTp_flat = hTp[:].rearrange("p a b -> p (a b)")
    A2 = Arena(hTp_flat, 32 * 1024)

    def hT(g0, n):
        t = g0 // 128
        keys = ["hT%d" % i for i in range(t, (g0 + n + 127) // 128)]
        if g0 < TPRE:
            return hTp[:, :, g0:g0 + n], keys
        return hTo[:, :, g0 - TPRE:g0 - TPRE + n], keys

    def mm(out, lhsT, rhs, start, stop, reads, writes, sgc=False):
        P.op("pe", lambda e: e.matmul(out, lhsT=lhsT, rhs=rhs, start=start, stop=stop, skip_group_check=sgc),
             reads, writes)

    def tr(out, in_, reads, writes):
        P.op("pe", lambda e: e.transpose(out=out, in_=in_, identity=ident[:]), list(reads) + ["const"], writes)

    def act(out, in_, func, reads, writes, bias=None, scale=None, accum=None):
        kw = {}
        if bias is not None:
            kw["bias"] = bias
        if scale is not None:
            kw["scale"] = scale
        if accum is not None:
            kw["accum_out"] = accum
        P.op("act", lambda e: e.activation(out=out, in_=in_, func=func, **kw), reads, writes)

    def tt(eng, out, in0, in1, op, reads, writes):
        P.op(eng, lambda e: e.tensor_tensor(out=out, in0=in0, in1=in1, op=op), reads, writes)

    def stt(out, in0, scalar, in1, op0, op1, reads, writes):
        P.op("dve", lambda e: e.scalar_tensor_tensor(out=out, in0=in0, scalar=scalar, in1=in1, op0=op0, op1=op1),
             reads, writes)

    def ts(eng, out, in0, s1, op0, reads, writes, s2=None, op1=None):
        if op1 is None:
            P.op(eng, lambda e: e.tensor_scalar(out=out, in0=in0, scalar1=s1, scalar2=None, op0=op0), reads, writes)
        else:
            P.op(eng, lambda e: e.tensor_scalar(out=out, in0=in0, scalar1=s1, scalar2=s2, op0=op0, op1=op1),
                 reads, writes)

    def cp(eng, out, in_, reads, writes):
        if eng == "act":
            P.op("act", lambda e: e.copy(out=out, in_=in_), reads, writes)
        else:
            P.op(eng, lambda e: e.tensor_copy(out=out, in_=in_), reads, writes)

    def dma(eng, out, in_, slot, reads, writes, nofence=False, **kw):
        P.op(eng, lambda e: e.dma_start(out=out, in_=in_, **kw), reads, writes, dma_slot=slot, nofence=nofence)

    wslot_rr = [0]

    def load_w(wd, col0, ncols=256, slot=None):
        if slot is None:
            slot = wslot_rr[0] % 4
            wslot_rr[0] += 1
        key = "wsl%d" % slot
        dma("pool", wsl[:, slot, :, 0:ncols], wd[:, col0:col0 + ncols].rearrange("(c p) n -> p c n", p=128),
            key, [], [key], nofence=True)
        return wsl[:, slot], key

    def pool_const(t, val_true_cmp, fill_else):
        pass

    P.op("pool", lambda e: e.memset(ident[:], 0.0), [], ["c_ident"])
    P.op("pool", lambda e: e.affine_select(out=ident[:], in_=ident[:], pattern=[[-1, 128]], compare_op=ALU.not_equal,
                                           fill=1.0, base=0, channel_multiplier=1), ["c_ident"], ["c_ident"])
    P.op("pool", lambda e: e.memset(ones[:], 1.0), [], ["c_ones"])
    P.op("pool", lambda e: e.affine_select(out=triS[:], in_=ones[:], pattern=[[-1, 128]], compare_op=ALU.is_gt,
                                           fill=0.0, base=0, channel_multiplier=1), ["c_ones"], ["c_triS"])
    P.op("pool", lambda e: e.affine_select(out=triI[:], in_=ones[:], pattern=[[1, 128]], compare_op=ALU.is_ge,
                                           fill=0.0, base=0, channel_multiplier=-1), ["c_ones"], ["c_triI"])
    P.op("pool", lambda e: e.affine_select(out=maskS[:], in_=ones[:], pattern=[[1, 128]], compare_op=ALU.is_gt,
                                           fill=0.0, base=0, channel_multiplier=-1), ["c_ones"], ["c_maskS"])
    P.op("pool", lambda e: e.affine_select(out=maskI[:], in_=ones[:], pattern=[[1, 128]], compare_op=ALU.is_ge,
                                           fill=0.0, base=0, channel_multiplier=-1), ["c_ones"], ["c_maskI"])
    dma("sp", gpre[:], g_pre_d.rearrange("(c p) -> p c", p=128), "c0", [], ["c_gpre"], allow_slow_non_contiguous=True)
    dma("sp", gmem[:], g_mem_d.rearrange("(c p) -> p c", p=128), "c1", [], ["c_gmem"], allow_slow_non_contiguous=True)
    dma("sp", ghead[:], ghead_d.partition_broadcast(128), "c2", [], ["c_ghead"])
    dma("pool", w2aug[0:16, :], w2_d, "c3", [], ["c_w2a"])
    dma("pool", w2aug[16:17, :], ab_d.rearrange("(o n) -> o n", o=1), "c4", [], ["c_w2b"])
    dma("pool", wga[:], w_in[:, C_GA:C_GA + 16].rearrange("(c p) n -> p c n", p=128), "c5", [], ["c_wga"])
    CONST = ["c_ident", "c_ones", "c_triS", "c_triI", "c_maskS", "c_maskI", "c_gpre", "c_gmem", "c_ghead",
             "c_w2a", "c_w2b", "c_wga"]
    P.barrier(full=True)

    def dump(name, src_ap, reads, rows=None):
        if name not in dbg_d:
            return
        dst = dbg_d[name]
        P.barrier(full=True)
        stg_full = arena[:, ARENA_B // 2 - 4096:ARENA_B // 2].bitcast(F32)
        n = int(np.prod(src_ap.shape[1:]))
        np_ = src_ap.shape[0]
        assert n <= 2048
        stg = stg_full[0:np_, 0:n]
        src2 = src_ap if len(src_ap.shape) == 2 else src_ap.rearrange("p a b -> p (a b)")
        if len(src_ap.shape) == 3:
            stg_v = stg.rearrange("p (a b) -> p a b", b=src_ap.shape[2])
            P.op("dve", lambda e: e.tensor_copy(out=stg_v, in_=src_ap), ["dbgstg"], ["dbgstg"])
        else:
            P.op("dve", lambda e: e.tensor_copy(out=stg, in_=src2), ["dbgstg"], ["dbgstg"])
        dma("sp", dst, stg, "dbg_" + name, ["dbgstg"], ["dbgout"])
        P.barrier(full=True)

    A.reset()
    xs = [A.take((128, D), F32) for _ in range(2)]
    xb = [A.take((128, D), BF16) for _ in range(2)]
    junk = A.take((128, D), BF16)

    def norm_tile(src_rows, i, dst_ap, dst_key, gvec, tag, bufs=None):
        s = i % 2
        kxs, kxb = "xs%d" % s, "xb%d" % s
        kps = ["ps%d" % (2 * s), "ps%d" % (2 * s + 1)]
        if bufs is None:
            xs_s, xb_s, junk_s = xs[s], xb[s], junk
        else:
            xs_s, xb_s, junk_s = bufs
        ss = small[:, 2 * s:2 * s + 1]
        rs = small[:, 2 * s + 1:2 * s + 2]
        kss = "ss%d" % s
        dma("sp", xs_s, src_rows, kxs, [], [kxs])
        act(junk_s, xs_s, AF.Square, [kxs], ["junk", kss], accum=ss)
        act(ss, ss, AF.Sqrt, [kss], [kss], bias=EPS, scale=1.0 / D)
        P.op("dve", lambda e: e.reciprocal(out=rs, in_=ss), [kss], [kss + "r"])
        ts("dve", xb_s[:, 0:D // 2], xs_s[:, 0:D // 2], rs, ALU.mult, [kxs, kss + "r"], [kxb + "a"])
        act(xb_s[:, D // 2:D], xs_s[:, D // 2:D], AF.Copy, [kxs, kss + "r"], [kxb + "b"], scale=rs)
        pview = bank_bf(2 * s, 2).rearrange("p (c t) -> p c t", t=128)
        for c in range(16):
            tr(pview[:, c, :], xb_s[:, c * 128:(c + 1) * 128], [kxb + ("a" if c < 8 else "b")], kps)
        tt("dve", dst_ap, pview, gvec[:, :].unsqueeze(2).to_broadcast([128, 16, 128]), ALU.mult,
           kps, [dst_key])

    for i in range(16):
        if i < 8:
            norm_tile(xp[i * 128:(i + 1) * 128, :], i, hTp[:, :, i * 128:(i + 1) * 128], "hT%d" % i, gpre, "x")
        else:
            j = i - 8
            norm_tile(xo[j * 128:(j + 1) * 128, :], i, hTo[:, :, j * 128:(j + 1) * 128], "hT%d" % i, gpre, "x")

    if dbg:
        dump("hT_own_c3", hTo[:, 3, :], [])
        dump("hT_pre_c0", hTp[:, 0, :], [])

    if stop_after <= 1:
        return finish(nc, P, stack, out_d, None)

    ipb = [0]

    def ipbank():
        b = ipb[0] % 2
        ipb[0] += 1
        return b

    def outproj_prefetch(wd, gate_col0, aliases=None):
        aliases = aliases or {}
        pre = {}
        srcs = [(wd, 0), (w_in, gate_col0), (wd, 256), (w_in, gate_col0 + 256)]
        got = []
        for s_, (src_d, col0) in enumerate(srcs):
            key = "wsl%d" % s_
            dma("pool", wsl[:, s_], src_d[:, col0:col0 + 256].rearrange("(c p) n -> p c n", p=128),
                key, [], [key] + list(aliases.get(s_, [])), nofence=True)
            got.append((wsl[:, s_], key))
        pre[0] = (got[0], got[1])
        pre[1] = (got[2], got[3])
        return pre

    P.barrier()
    A.reset()
    gk8 = [A.take((128, 8, 256), BF16) for _ in range(2)]
    gv8 = [A.take((128, 8, 512), BF16) for _ in range(2)]
    sp8 = [A.take((128, 8, 256), BF16) for _ in range(2)]
    gqT = A.take((128, 2, TOWN), BF16)
    gkT = A.take((128, 2, TOWN), BF16)
    ggs = A.take((128, 8, 512), BF16)
    S32 = A.take((128, 2, 512), F32)
    Sbf = A.take((128, 2, 512), BF16)
    ue = A.take((128, 256), F32)
    edb = [A.take((128, 256), F32) for _ in range(2)]
    kdb = [A.take((128, 256), BF16) for _ in range(2)]
    ebTb = [A.take((128, 2, 128), F32) for _ in range(2)]
    ekTb = [A.take((128, 2, 128), F32) for _ in range(2)]
    qgTb = [A.take((128, 2, 128), BF16) for _ in range(2)]
    kgTb = [A.take((128, 2, 128), BF16) for _ in range(2)]
    attn = A.take((128, 128), BF16)
    o_n = A.take((128, 512), F32)
    o_a = A.take((128, 512), BF16)
    gaT = A.take((17, TPRE + TOWN), BF16)
    P.op("pool", lambda e: e.memset(gaT, 1.0), [], ["gaT"])
    ssq = small[:, 8:9]
    rq = small[:, 9:10]

    for tb in range(4):
        hap, hk = hT(tb * 512, 512)
        b = ipbank()
        for c in range(16):
            mm(bank(b)[0:16, :], wga[:, c, :], hap[:, c, :], c == 0, c == 15, hk, ["ps%d" % b])
        cp("dve", gaT[0:16, tb * 512:(tb + 1) * 512], bank(b)[0:16, :], ["ps%d" % b], ["gaT"])

    def gla_rec_stations(hd, half, j):
        own = half == 1
        bs = half
        p_ = j % 2
        ed, kd, ebT, ekT, qgT, kgT = edb[p_], kdb[p_], ebTb[p_], ekTb[p_], qgTb[p_], kgTb[p_]
        k_ed, k_kd, k_ebT, k_ekT, k_qgT, k_kgT = ("%s%d" % (n_, p_) for n_ in ("ed", "kd", "ebT", "ekT", "qgT", "kgT"))
        tok = slice(j * 128, (j + 1) * 128)
        spj = sp8[bs][:, j, :]
        ksp, kgk, kgv = "sp8_%d_%d" % (bs, j), "gk8_%d_%d" % (bs, j), "gv8_%d_%d" % (bs, j)
        ps3v = bank(2)[:, 256:512].rearrange("p (a b) -> p a b", b=128)

        def st1():
            mm(bank(2)[:, 0:256], triS[:], spj, True, True, [ksp], ["ps2"])
            for cc in range(2):
                mm(bank(2)[:, 256 + cc * 128:256 + (cc + 1) * 128], spj[:, cc * 128:(cc + 1) * 128], triI[:],
                   True, True, [ksp], ["ps2"])
            act(ed, bank(2)[:, 0:256], AF.Exp, ["ps2"], [k_ed], scale=-1.0 / 16)
            act(ebT, ps3v, AF.Exp, ["ps2"], [k_ebT], scale=-1.0 / 16)
            if own:
                act(ekT, ps3v, AF.Exp, ["ps2"], [k_ekT], scale=1.0 / 16)
            tt("dve", kd, gk8[bs][:, j, :], ed, ALU.mult, [kgk, k_ed], [k_kd])
            if own:
                stt(qgT, gqT[:, :, tok], 1.0 / 16, ebT, ALU.mult, ALU.mult, ["gqT", k_ebT], [k_qgT])
                tt("dve", kgT, gkT[:, :, tok], ekT, ALU.mult, ["gkT", k_ekT], [k_kgT])

        def st2():
            if own:
                for cc in range(2):
                    mm(bank(4)[:, 0:128], kgT[:, cc, :], qgT[:, cc, :], cc == 0, cc == 1, [k_kgT, k_qgT], ["ps4"])
            for cc in range(2):
                mm(bank(6 + cc), kd[:, cc * 128:(cc + 1) * 128], gv8[bs][:, j, :], True, True, [k_kd, kgv],
                   ["ps%d" % (6 + cc)])
            if own:
                tt("dve", attn, bank(4)[:, 0:128], maskI[:], ALU.mult, ["ps4"], ["attn"])

        def st3():
            if own:
                mm(bank(5), attn, gv8[bs][:, j, :], True, False, ["attn", kgv], ["ps5"])
                for cc in range(2):
                    mm(bank(5), qgT[:, cc, :], Sbf[:, cc, :], False, cc == 1, [k_qgT, "Sbf%d" % cc], ["ps5"])
                act(o_a, bank(5), AF.Square, ["ps5"], ["o_a", "ssq"], accum=ssq)
                act(ssq, ssq, AF.Ln, ["ssq"], ["ssq"], bias=EPS, scale=1.0 / 512)
                act(rq, ssq, AF.Exp, ["ssq"], ["rq"], scale=-0.5)
                stt(o_n, bank(5), rq, ghead[:], ALU.mult, ALU.mult, ["ps5", "rq"], ["o_n"])

        def st4():
            for cc in range(2):
                stt(S32[:, cc, :], S32[:, cc, :], ebT[:, cc, 127:128], bank(6 + cc), ALU.mult, ALU.add,
                    ["S32_%d" % cc, k_ebT, "ps%d" % (6 + cc)], ["S32_%d" % cc])
                cp("act", Sbf[:, cc, :], S32[:, cc, :], ["S32_%d" % cc], ["Sbf%d" % cc])
            if own:
                tt("dve", o_a, o_n, ggs[:, j, :], ALU.mult, ["o_n", "ggs"], ["o_a"])

        def st5():
            if own:
                pv = bank_bf(3)[:, 0:512].rearrange("p (a b) -> p a b", b=128)
                for dvc in range(4):
                    tr(pv[:, dvc, :], o_a[:, dvc * 128:(dvc + 1) * 128], ["o_a"], ["ps3"])
                cp("dve", oT[:, hd * 4:(hd + 1) * 4, tok], pv, ["ps3"], ["oT"])

        return [st1, st2, st3, st4, st5]

    def gla_ip_chunks(hd, half, j, W):
        bs = half
        g0 = half * 1024 + j * 128
        hap, hk = hT(g0, 128)
        (wk, kwk), (wv0, kwv0), (wv1, kwv1) = W
        st = {}

        def c1a():
            st["bk"] = ipbank()
            b = st["bk"]
            for c in range(8):
                mm(bank(b)[:, 0:256], hap[:, c, :], wk[:, c, :], c == 0, c == 15, hk + [kwk], ["ps%d" % b])

        def c1b():
            b = st["bk"]
            for c in range(8, 16):
                mm(bank(b)[:, 0:256], hap[:, c, :], wk[:, c, :], c == 0, c == 15, hk + [kwk], ["ps%d" % b])
            cp("act", gk8[bs][:, j, :], bank(b)[:, 0:256], ["ps%d" % b], ["gk8_%d_%d" % (bs, j)])

        def c2a():
            st["b"] = ipbank()
            b = st["b"]
            for c in range(8):
                mm(bank(b)[:, 0:256], hap[:, c, :], wv0[:, c, :], c == 0, c == 15, hk + [kwv0], ["ps%d" % b])

        def c2b():
            b = st["b"]
            for c in range(8, 16):
                mm(bank(b)[:, 0:256], hap[:, c, :], wv0[:, c, :], c == 0, c == 15, hk + [kwv0], ["ps%d" % b])

        def c3a():
            b = st["b"]
            for c in range(8):
                mm(bank(b)[:, 256:512], hap[:, c, :], wv1[:, c, :], c == 0, c == 15, hk + [kwv1], ["ps%d" % b])

        def c3b():
            b = st["b"]
            for c in range(8, 16):
                mm(bank(b)[:, 256:512], hap[:, c, :], wv1[:, c, :], c == 0, c == 15, hk + [kwv1], ["ps%d" % b])
            cp("dve", gv8[bs][:, j, :], bank(b), ["ps%d" % b], ["gv8_%d_%d" % (bs, j)])

        def c1t():
            b = ipbank()
            pvt = bank_bf(b)[:, 0:256].rearrange("p (a b) -> p a b", b=128)
            for cc in range(2):
                tr(pvt[:, cc, :], gk8[bs][:, j, cc * 128:(cc + 1) * 128], ["gk8_%d_%d" % (bs, j)], ["ps%d" % b])
            cp("dve", gkT[:, :, j * 128:(j + 1) * 128], pvt, ["ps%d" % b], ["gkT"])

        def c4():
            b = ipbank()
            mm(bank(b)[:, 0:256], gaT[0:17, g0:g0 + 128], w2aug[0:17, hd * 256:(hd + 1) * 256], True, True,
               ["gaT"], ["ps%d" % b])
            act(ue, bank(b)[:, 0:256], AF.Exp, ["ps%d" % b], ["ue"], scale=-1.0)
            act(sp8[bs][:, j, :], ue, AF.Ln, ["ue"], ["sp8_%d_%d" % (bs, j)], bias=1.0)

        if half == 1:
            return [c1a, c1b, c2a, c2b, c1t, c3a, c3b, c4]
        return [c1a, c1b, c2a, c2b, c3a, c3b, c4]

    def gla_load_kv(hd):
        return (load_w(w_in, C_GK + hd * 256, slot=0), load_w(w_in, C_GV + hd * 512, slot=1),
                load_w(w_in, C_GV + hd * 512 + 256, slot=2))

    def reorder(S):
        out = [S[0][0]]
        for j in range(len(S)):
            out.append(S[j][1])
            if j + 1 < len(S):
                out.append(S[j + 1][0])
            out += S[j][2:]
        return out

    def interleave(rec_list, fill_list, skip):
        gaps = len(rec_list)
        nfill = len(fill_list)
        fi = 0
        for gi, st in enumerate(rec_list):
            st()
            if gi < skip or nfill == 0:
                continue
            remaining_gaps = gaps - gi
            n = -(-(nfill - fi) // remaining_gaps)
            for _ in range(n):
                if fi < nfill:
                    fill_list[fi]()
                    fi += 1
        while fi < nfill:
            fill_list[fi]()
            fi += 1

    Wkv = gla_load_kv(0)
    wq, kwq = load_w(w_in, C_GQ, slot=3)
    for ch in [c for j in range(8) for c in gla_ip_chunks(0, 0, j, Wkv)]:
        ch()
    for hd in range(4):
        wk, kwk = Wkv[0]
        for (wsrc, kw, dst, kdst) in ((wq, kwq, gqT, "gqT"),):
            for cc in range(2):
                for tb in range(2):
                    hap, hk = hT(TPRE + tb * 512, 512)
                    b = ipbank()
                    for c in range(16):
                        mm(bank(b), wsrc[:, c, cc * 128:(cc + 1) * 128], hap[:, c, :], c == 0, c == 15,
                           hk + [kw], ["ps%d" % b])
                    cp("act", dst[:, cc, tb * 512:(tb + 1) * 512], bank(b), ["ps%d" % b], [kdst])
        for gh in range(2):
            wg, kwg = load_w(w_in, C_GG + hd * 512 + gh * 256, slot=3)
            for j in range(8):
                hap, hk = hT(TPRE + j * 128, 128)
                b = ipbank()
                for c in range(16):
                    mm(bank(b)[:, 0:256], hap[:, c, :], wg[:, c, :], c == 0, c == 15, hk + [kwg], ["ps%d" % b])
                act(ggs[:, j, gh * 256:(gh + 1) * 256], bank(b)[:, 0:256], AF.Silu, ["ps%d" % b], ["ggs"])
        if hd + 1 < 4:
            wq, kwq = load_w(w_in, C_GQ + (hd + 1) * 256, slot=3)
        P.op("pool", lambda e: e.memset(S32[:], 0.0), [], ["S32_0", "S32_1"])
        P.op("pool", lambda e: e.memset(Sbf[:], 0.0), [], ["Sbf0", "Sbf1"])
        rec = reorder([gla_rec_stations(hd, 0, j) for j in range(8)])
        fill = [c for j in range(8) for c in gla_ip_chunks(hd, 1, j, Wkv)]
        interleave(rec, fill, 0)
        rec = reorder([gla_rec_stations(hd, 1, j) for j in range(8)])
        if hd + 1 < 4:
            Wkv = gla_load_kv(hd + 1)
            fill = [c for j in range(8) for c in gla_ip_chunks(hd + 1, 0, j, Wkv)]
            interleave(rec, fill, 12)
        else:
            pre_a = outproj_prefetch(w_pa, C_GATE)
            interleave(rec, [], 0)


    if dbg:
        dump("oT_a0", oT[:, 0, :], [])
        dump("oT_a9", oT[:, 9, :], [])
    if stop_after <= 2:
        return finish(nc, P, stack, out_d, None)

    P.barrier()
    A.reset()
    mergedT = A.take((128, 16, TOWN), BF16)
    MERGED_END = A.off
    opc = [0]

    def outproj_setup():
        A.reset(MERGED_END)
        sg_ = [A.take((128, 512), F32) for _ in range(2)]
        pt_ = [A.take((128, 512), F32) for _ in range(2)]
        xs_ = [A.take((128, 16, 256), BF16) for _ in range(3)]
        ring = [(wsl[:, i], "wsl%d" % i) for i in range(4)] + [(xs_[i], "xw%d" % i) for i in range(3)]
        return sg_, pt_, ring

    def outproj(wd, gate_col0, first, setup, pre=None):
        sgm, ptmp, ring = setup
        rr = [4 if pre else 0]

        def ld(src_d, col0):
            ap, key = ring[rr[0] % len(ring)]
            rr[0] += 1
            dma("pool", ap[:, :, 0:256], src_d[:, col0:col0 + 256].rearrange("(c p) n -> p c n", p=128),
                key, [], [key])
            return ap, key

        loaded = dict(pre) if pre else {}

        def ensure(i):
            if i < 8 and i not in loaded:
                loaded[i] = (ld(wd, i * 256), ld(w_in, gate_col0 + i * 256))

        ensure(0)
        ensure(1)
        for cb2 in range(8):
            ensure(cb2 + 2)
            (wp, kwp), (wg, kwg) = loaded[cb2]
            for sub in range(2):
                cb = cb2 * 2 + sub
                for tb in range(2):
                    i = opc[0] % 2
                    opc[0] += 1
                    pb, gb = i, 2 + i
                    tsl = slice(tb * 512, (tb + 1) * 512)
                    for k in range(16):
                        mm(bank(pb), wp[:, k, sub * 128:(sub + 1) * 128], oT[:, k, tsl], k == 0, k == 15,
                           ["oT", kwp], ["ps%d" % pb])
                    hap, hk = hT(TPRE + tb * 512, 512)
                    for c in range(16):
                        mm(bank(gb), wg[:, c, sub * 128:(sub + 1) * 128], hap[:, c, :], c == 0, c == 15,
                           hk + [kwg], ["ps%d" % gb])
                    act(sgm[i], bank(gb), AF.Sigmoid, ["ps%d" % gb], ["sgm%d" % i])
                    if first:
                        tt("dve", mergedT[:, cb, tsl], bank(pb), sgm[i], ALU.mult, ["ps%d" % pb, "sgm%d" % i],
                           ["mT%d" % cb])
                    else:
                        tt("dve", ptmp[i], bank(pb), sgm[i], ALU.mult, ["ps%d" % pb, "sgm%d" % i], ["ptmp%d" % i])
                        tt("dve", mergedT[:, cb, tsl], mergedT[:, cb, tsl], ptmp[i], ALU.add,
                           ["mT%d" % cb, "ptmp%d" % i], ["mT%d" % cb])

    outproj(w_pa, C_GATE, True, outproj_setup(), pre_a)
    if dbg:
        dump("mT_a0", mergedT[:, 0, :], [])
        dump("mT_a7", mergedT[:, 7, :], [])
    if stop_after <= 3:
        return finish(nc, P, stack, out_d, None)

    P.barrier()
    A.reset(MERGED_END)
    skT = [A.take((128, TPRE + TOWN), BF16) for _ in range(2)]
    sqT = [A.take((128, TOWN), BF16) for _ in range(2)]
    sgs = [A.take((128, TOWN), BF16) for _ in range(2)]
    sv2 = A.take((128, 16, 256), BF16)
    Lsum = [A.take((128, 512), BF16) for _ in range(2)]
    Eb = [A.take((128, 512), F32) for _ in range(4)]
    SPb = [A.take((128, 512), BF16) for _ in range(2)]
    Wtb = [A.take((128, 512), BF16) for _ in range(2)]
    sil = A.take((128, 512), F32)
    SCALE = 1.0 / math.sqrt(128.0)
    gstep = [0]
    ggrp = [0]
    ZB = (2, 3, 4)
    AB = (5, 6)
    OB = (7, 0)

    def sb_ipbank():
        b = 2 + ipb[0] % 2
        ipb[0] += 1
        return b

    SUBK = 4

    def sb_ip_chunks(hh, par, W, bankfn):
        (wsq, kwsq), (wsk, kwsk), (wsg, kwsg) = W
        hs = slice(0, 128)
        chunks = []

        def mk(wt, kw, g0, kind, dst, kdst):
            st = {}
            subs = []
            for q in range(16 // SUBK):
                def f(q=q):
                    hap, hk = hT(g0, 512)
                    if q == 0:
                        st["b"] = bankfn()
                    b = st["b"]
                    for c in range(q * SUBK, (q + 1) * SUBK):
                        mm(bank(b), wt[:, c, hs], hap[:, c, :], c == 0, c == 15, hk + [kw], ["ps%d" % b])
                    if q == 16 // SUBK - 1:
                        if kind == "act":
                            cp("act", dst, bank(b), ["ps%d" % b], [kdst])
                        elif kind == "dve":
                            cp("dve", dst, bank(b), ["ps%d" % b], [kdst])
                        else:
                            act(sil, bank(b), AF.Exp, ["ps%d" % b], ["sil"], scale=-1.0)
                            act(sil, sil, AF.Ln, ["sil"], ["sil"], bias=1.0)
                            act(sil, sil, AF.Exp, ["sil"], ["sil"], scale=-1.0)
                            tt("dve", dst, bank(b), sil, ALU.mult, ["ps%d" % b, "sil"], [kdst])
                subs.append(f)
            return subs

        for tb in range(4):
            chunks += mk(wsk, kwsk, tb * 512, "act" if tb % 2 == 0 else "dve",
                         skT[par][:, tb * 512:(tb + 1) * 512], "skT%d" % par)
        for tb in range(2):
            chunks += mk(wsq, kwsq, TPRE + tb * 512, "dve", sqT[par][:, tb * 512:(tb + 1) * 512],
                         "sqT%d" % par)
            chunks += mk(wsg, kwsg, TPRE + tb * 512, "silu", sgs[par][:, tb * 512:(tb + 1) * 512],
                         "sgs%d" % par)
        return chunks

    def sb_pair(hp, fillA, fillB, load_at_start, load_mid):
        steps = []
        for hh in range(2):
            for qg in range(2):
                nkb = 12 + 4 * qg
                g = ggrp[0]
                ggrp[0] += 1
                for gi, kb in enumerate(reversed(range(nkb))):
                    r0 = max(0, kb - 8 - 4 * qg)
                    n = gstep[0]
                    gstep[0] += 1
                    steps.append(dict(n=n, qg=qg, kb=kb, gi=gi, g=g, r0=r0, diag=kb >= 8 + 4 * qg,
                                      first=gi == 0, last=kb == 0, hh=hh, par=hh, head=hp * 2 + hh))

        def stage(st, sg):
            n, r0, gi, par, hh = st["n"], st["r0"], st["gi"], st["par"], st["hh"]
            kskT, ksqT, ksgs = "skT%d" % par, "sqT%d" % par, "sgs%d" % par
            zb, ab, ob = ZB[n % 3], AB[n % 2], OB[st["g"] % 2]
            E, kE = Eb[n % 4], "E%d" % (n % 4)
            SP, kSP = SPb[n % 2], "SPb%d" % (n % 2)
            Wt, kW = Wtb[n % 2], "Wt%d" % (n % 2)
            cols = slice(r0 * 128, 512)
            dcols = slice(r0 * 128, (r0 + 1) * 128)
            rcols = slice((r0 + 1) * 128, 512)
            Lold, kLold = Lsum[gi % 2], "Ls%d" % (gi % 2)
            Lnew, kLnew = Lsum[(gi + 1) % 2], "Ls%d" % ((gi + 1) % 2)
            lc = rcols if st["diag"] else cols
            has_l = (not st["first"]) and (not st["diag"] or r0 < 3)
            kz, ka, ko = "ps%d" % zb, "ps%d" % ab, "ps%d" % ob
            qg, kb = st["qg"], st["kb"]
            if sg == 0:
                mm(bank(zb)[:, cols], skT[par][:, kb * 128:(kb + 1) * 128],
                   sqT[par][:, qg * 512 + r0 * 128:(qg + 1) * 512], True, True, [kskT, ksqT], [kz])
            elif sg == 1:
                act(E[:, cols], bank(zb)[:, cols], AF.Exp, [kz], [kE], scale=SCALE)
                act(E[:, cols], E[:, cols], AF.Ln, [kE], [kE], bias=1.0)
            elif sg == 2:
                if st["diag"]:
                    tt("dve", SP[:, dcols], E[:, dcols], maskS[:], ALU.mult, [kE], [kSP])
                    if r0 < 3:
                        cp("dve", SP[:, rcols], E[:, rcols], [kE], [kSP])
                else:
                    cp("dve", SP[:, cols], E[:, cols], [kE], [kSP])
                stt(E[:, cols], bank(zb)[:, cols], SCALE, E[:, cols], ALU.mult, ALU.subtract, [kz, kE], [kE])
                if st["diag"]:
                    cp("pool", Lnew[:, dcols], SP[:, dcols], [kSP], [kLnew])
                if has_l:
                    tt("pool", Lnew[:, lc], Lold[:, lc], SP[:, lc], ALU.add, [kLold, kSP], [kLnew])
            elif sg == 3:
                mm(bank(ab)[:, cols], triS[:], SP[:, cols], True, not has_l, [kSP], [ka])
                if has_l:
                    mm(bank(ab)[:, lc], ones[:], Lold[:, lc], False, True, [kLold], [ka])
            elif sg == 4:
                tt("dve", E[:, cols], E[:, cols], bank(ab)[:, cols], ALU.subtract, [kE, ka], [kE])
            elif sg == 5:
                act(Wt[:, cols], E[:, cols], AF.Exp, [kE], [kW])
                if st["diag"]:
                    tt("pool", Wt[:, dcols], Wt[:, dcols], maskS[:], ALU.mult, [kW], [kW])
            elif sg == 6:
                mm(bank(ob)[:, cols], sv2[:, kb, hh * 128:(hh + 1) * 128], Wt[:, cols], st["first"], st["last"],
                   ["sv2", kW], [ko], sgc=True)
                if st["last"]:
                    tt("dve", oT[:, st["head"], qg * 512:(qg + 1) * 512], bank(ob),
                       sgs[par][:, qg * 512:(qg + 1) * 512], ALU.mult, [ko, ksgs], ["oT"])

        NST = 7
        fa = 0
        fb = 0
        if load_at_start is not None:
            load_at_start()
        for slot in range(len(steps) + NST - 1):
            for sg in reversed(range(NST)):
                i = slot - sg
                if 0 <= i < len(steps):
                    stage(steps[i], sg)
            if fa < len(fillA) and slot <= 31:
                nq = -(-(len(fillA) - fa) // (32 - slot))
                for _ in range(nq):
                    fillA[fa]()
                    fa += 1
            if slot == 32:
                assert fa == len(fillA)
                if load_mid is not None:
                    load_mid()
            if slot >= 32 and fb < len(fillB):
                nq = -(-(len(fillB) - fb) // max(1, 61 - slot))
                for _ in range(nq):
                    if fb < len(fillB):
                        fillB[fb]()
                        fb += 1
        assert fa == len(fillA)
        while fb < len(fillB):
            fillB[fb]()
            fb += 1

    HS = [(wsl[:, s_, :, h_ * 128:(h_ + 1) * 128], "wh%d_%d" % (s_, h_)) for s_ in (0, 1, 3) for h_ in (0, 1)]
    grpA = [HS[0], HS[2], HS[4]]
    grpB = [HS[1], HS[3], HS[5]]
    first_load = [True]

    def ld_half(slotkey, col0):
        ap, key = slotkey
        nf = not first_load[0]
        first_load[0] = False
        dma("pool", ap, w_in[:, col0:col0 + 128].rearrange("(c p) n -> p c n", p=128), key, [], [key], nofence=nf)
        return ap, key

    def sb_load_head(hp, hh, grp):
        c = hp * 256 + hh * 128
        return (ld_half(grp[0], C_SQ + c), ld_half(grp[1], C_SK + c), ld_half(grp[2], C_SG + c))

    def sb_load_sv(hp):
        first_load[0] = False
        return load_w(w_in, C_SV + hp * 256, slot=2)

    WB = sb_load_head(0, 0, grpB)
    wsv_cur = sb_load_sv(0)
    WA = sb_load_head(0, 1, grpA)
    for ch in sb_ip_chunks(0, 0, WB, sb_ipbank):
        ch()
    for hp in range(8):
        wsv, kwsv = wsv_cur
        for i in range(16):
            hap, hk = hT(i * 128, 128)
            b = sb_ipbank()
            for c in range(16):
                mm(bank(b)[:, 0:256], hap[:, c, :], wsv[:, c, :], c == 0, c == 15, hk + [kwsv], ["ps%d" % b])
            cp("act" if i % 2 == 0 else "dve", sv2[:, i, :], bank(b)[:, 0:256], ["ps%d" % b], ["sv2"])
        fillA = sb_ip_chunks(1, 1, WA, lambda: 1)
        if hp + 1 < 8:
            st_ = {}

            def load_start(hp=hp, st_=st_):
                st_["sv"] = sb_load_sv(hp + 1)
                st_["WB"] = sb_load_head(hp + 1, 0, grpB)

            def load_mid(hp=hp, st_=st_):
                st_["WA"] = sb_load_head(hp + 1, 1, grpA)

            class _Lazy(list):
                pass
            load_start()
            fillB = sb_ip_chunks(0, 0, st_["WB"], lambda: 1)
            sb_pair(hp, fillA, fillB, None, load_mid)
            wsv_cur, WA = st_["sv"], st_["WA"]
        else:
            pre_b_box = {}

            def load_mid_last():
                al = {0: ["wh0_0", "wh0_1"], 1: ["wh1_0", "wh1_1"], 3: ["wh3_0", "wh3_1"]}
                pre_b_box["pre"] = outproj_prefetch(w_pb, C_GATE + 2048, al)

            sb_pair(hp, fillA, [], None, load_mid_last)

    if dbg:
        dump("oT_b0", oT[:, 0, :], [])
        dump("oT_b13", oT[:, 13, :], [])
    if stop_after <= 4:
        return finish(nc, P, stack, out_d, None)

    P.barrier()
    outproj(w_pb, C_GATE + 2048, False, outproj_setup(), pre_b_box["pre"])
    if dbg:
        dump("mT_ab5", mergedT[:, 5, :], [])
    if stop_after <= 5:
        return finish(nc, P, stack, out_d, None)

    P.barrier()
    A.reset(MERGED_END)
    A2.reset()
    mxs = A.take((128, D), F32)
    mxb = A.take((128, D), BF16)
    mjunk = A.take((128, D), BF16)
    sce = A.take((128, 256), F32)
    Pn = A.take((128, 256), BF16)
    pT = A.take((128, 2, 128), BF16)
    mhT = A2.take((128, 16, 256), BF16)
    mkT = A2.take((128, 16, 256), BF16)
    mv = A2.take((128, 2, D), BF16)
    mqT = A2.take((128, 4, TOWN), BF16)
    for i in range(2):
        norm_tile(memb[i * 128:(i + 1) * 128, :], 0, mhT[:, :, i * 128:(i + 1) * 128], "mhT", gmem, "m",
                  bufs=(mxs, mxb, mjunk))
    for cb2 in range(8):
        wmk, kw = load_w(w_mem_kv, cb2 * 256)
        for sub in range(2):
            b = ipbank()
            for c in range(16):
                mm(bank(b)[:, 0:256], wmk[:, c, sub * 128:(sub + 1) * 128], mhT[:, c, :], c == 0, c == 15,
                   ["mhT", kw], ["ps%d" % b])
            cp("act", mkT[:, cb2 * 2 + sub, :], bank(b)[:, 0:256], ["ps%d" % b], ["mkT"])
    for cb2 in range(8):
        wmv, kw = load_w(w_mem_kv, 2048 + cb2 * 256)
        for mt in range(2):
            b = ipbank()
            for c in range(16):
                mm(bank(b)[:, 0:256], mhT[:, c, mt * 128:(mt + 1) * 128], wmv[:, c, :], c == 0, c == 15,
                   ["mhT", kw], ["ps%d" % b])
            cp("dve", mv[:, mt, cb2 * 256:(cb2 + 1) * 256], bank(b)[:, 0:256], ["ps%d" % b], ["mv"])
    MSCALE = 1.0 / math.sqrt(512.0)
    mqT2 = [mqT, A.take((128, 4, TOWN), BF16)]
    sceb = [sce, A.take((128, 256), F32)]
    Pnb = [Pn, A.take((128, 256), BF16)]
    pTb = [pT, A.take((128, 2, 128), BF16)]

    def mq_chunks(hd, W2):
        par = hd % 2
        subs = []
        for half in range(2):
            wmq, kw = W2[half]
            for sub in range(2):
                cc = half * 2 + sub
                for tb in range(2):
                    st = {}
                    for q in range(4):
                        def f(q=q, st=st, wmq=wmq, kw=kw, sub=sub, cc=cc, tb=tb):
                            hap, hk = hT(TPRE + tb * 512, 512)
                            if q == 0:
                                st["b"] = ipbank()
                            b = st["b"]
                            for c in range(q * 4, (q + 1) * 4):
                                mm(bank(b), wmq[:, c, sub * 128:(sub + 1) * 128], hap[:, c, :], c == 0, c == 15,
                                   hk + [kw], ["ps%d" % b])
                            if q == 3:
                                cp("act", mqT2[par][:, cc, tb * 512:(tb + 1) * 512], bank(b), ["ps%d" % b],
                                   ["mqT%d" % par])
                        subs.append(f)
        return subs

    def mq_load(hd):
        s0 = (hd % 2) * 2
        return [load_w(w_in, C_MQ + hd * 512, slot=s0), load_w(w_in, C_MQ + hd * 512 + 256, slot=s0 + 1)]

    its = [(hd, j) for hd in range(4) for j in range(8)]

    def mstage(n, sg):
        hd, j = its[n]
        par = hd % 2
        i2 = n % 2
        tok = slice(j * 128, (j + 1) * 128)
        sb_, tb_, ob_ = 2 + i2, 4 + i2, 6 + i2
        ks, kt, ko = "ps%d" % sb_, "ps%d" % tb_, "ps%d" % ob_
        mx, nb, sm, rs = (small[:, 16 + 4 * i2 + q:17 + 4 * i2 + q] for q in range(4))
        kq = "msc%d" % i2
        sce_, Pn_, pT_ = sceb[i2], Pnb[i2], pTb[i2]
        ptv = bank_bf(tb_)[:, 0:256].rearrange("p (a b) -> p a b", b=128)
        if sg == 0:
            for cc in range(4):
                mm(bank(sb_)[:, 0:256], mqT2[par][:, cc, tok], mkT[:, hd * 4 + cc, :], cc == 0, cc == 3,
                   ["mqT%d" % par, "mkT"], [ks])
        elif sg == 1:
            P.op("dve", lambda e: e.tensor_reduce(out=mx, in_=bank(sb_)[:, 0:256], axis=AX.X, op=ALU.max),
                 [ks], [kq + "mx"])
            ts("dve", nb, mx, -MSCALE, ALU.mult, [kq + "mx"], [kq + "nb"])
            act(sce_, bank(sb_)[:, 0:256], AF.Exp, [ks, kq + "nb"], ["sce%d" % i2, kq + "sm"], bias=nb,
                scale=MSCALE, accum=sm)
            P.op("dve", lambda e: e.reciprocal(out=rs, in_=sm), [kq + "sm"], [kq + "rs"])
            ts("dve", Pn_, sce_, rs, ALU.mult, ["sce%d" % i2, kq + "rs"], ["Pn%d" % i2])
        elif sg == 2:
            for mt in range(2):
                tr(ptv[:, mt, :], Pn_[:, mt * 128:(mt + 1) * 128], ["Pn%d" % i2], [kt])
            cp("act", pT_, ptv, [kt], ["pT%d" % i2])
        elif sg == 3:
            for dvc in range(4):
                for mt in range(2):
                    mm(bank(ob_)[:, dvc * 128:(dvc + 1) * 128],
                       mv[:, mt, hd * 512 + dvc * 128:hd * 512 + (dvc + 1) * 128], pT_[:, mt, :], mt == 0, mt == 1,
                       ["mv", "pT%d" % i2], [ko])
            cp("dve", oT[:, hd * 4:(hd + 1) * 4, tok], bank(ob_).rearrange("p (a b) -> p a b", b=128), [ko], ["oT"])

    Wq = mq_load(0)
    for f in mq_chunks(0, Wq):
        f()
    Wq_next = mq_load(1)
    fillers = []
    NSTM = 4
    for slot in range(len(its) + NSTM - 1):
        if slot < len(its) and slot % 8 == 0:
            hd = slot // 8
            if hd + 1 < 4:
                fillers = mq_chunks(hd + 1, Wq_next)
                if hd + 2 < 4:
                    Wq_next = mq_load(hd + 2)
            else:
                fillers = []
                pre_m = outproj_prefetch(w_pm, C_GATE + 4096)
            fpos = 0
        for sg in reversed(range(NSTM)):
            n = slot - sg
            if 0 <= n < len(its):
                mstage(n, sg)
        if slot < len(its):
            left = 8 - slot % 8
            nq = -(-(len(fillers) - fpos) // left)
            for _ in range(nq):
                if fpos < len(fillers):
                    fillers[fpos]()
                    fpos += 1
    if dbg:
        dump("oT_m2", oT[:, 2, :], [])
        dump("oT_m15", oT[:, 15, :], [])
    if stop_after <= 6:
        return finish(nc, P, stack, out_d, None)

    P.barrier()
    hTo_flat = hTo[:].rearrange("p a b -> p (a b)")
    wo_lo = hTp_flat.rearrange("p (k n) -> p k n", n=D)
    wo_hi = hTo_flat.rearrange("p (k n) -> p k n", n=D)
    for k in range(8):
        dma("pool", wo_lo[:, k, :], w_out[k * 128:(k + 1) * 128, :], "wo%d" % k, [],
            ["wo%d" % k, "mhT", "mkT", "mv", "mqT"], max_dma_last_dim=4096)
    outproj(w_pm, C_GATE + 4096, False, outproj_setup(), pre_m)
    if dbg:
        dump("mT_f5", mergedT[:, 5, :], [])
    if stop_after <= 7:
        return finish(nc, P, stack, out_d, None)

    P.barrier()
    A.reset(MERGED_END)
    xs2 = [A.take((128, D), F32) for _ in range(2)]
    gpost = A.take((128, D), F32)
    oT_flat = oT[:].rearrange("p a b -> p (a b)")
    A3 = Arena(oT_flat, 32 * 1024)
    yo = [A3.take((128, D), F32) for _ in range(2)]
    dma("sp", gpost, g_post_d.partition_broadcast(128), "c6", [], ["gpost"])
    for k in range(8, 16):
        dma("pool", wo_hi[:, k - 8, :], w_out[k * 128:(k + 1) * 128, :], "wo%d" % k, [], ["wo%d" % k],
            max_dma_last_dim=4096)
    ss8, rs8 = small[:, 24:25], small[:, 25:26]
    for j in range(8):
        s = j % 2
        tok = slice(j * 128, (j + 1) * 128)
        dma("sp", xs2[s], xo[tok, :], "xs2_%d" % s, [], ["xs2_%d" % s])
        for cb in range(4):
            b = 4 * s + cb
            for k in range(16):
                wk_ = wo_lo[:, k, cb * 512:(cb + 1) * 512] if k < 8 else wo_hi[:, k - 8, cb * 512:(cb + 1) * 512]
                mm(bank(b), mergedT[:, k, tok], wk_, k == 0, k == 15, ["mT%d" % k, "wo%d" % k], ["ps%d" % b])
        yv = ps[:, 4 * s:4 * s + 4, :].rearrange("p b n -> p (b n)")
        kpy = ["ps%d" % (4 * s + q) for q in range(4)]
        act(yo[s], yv, AF.Square, kpy, ["yo%d" % s, "ss8"], accum=ss8)
        act(ss8, ss8, AF.Sqrt, ["ss8"], ["ss8"], bias=EPS, scale=1.0 / D)
        P.op("dve", lambda e: e.reciprocal(out=rs8, in_=ss8), ["ss8"], ["rs8"])
        stt(yo[s], yv, rs8, gpost, ALU.mult, ALU.mult, kpy + ["rs8", "gpost"], ["yo%d" % s])
        tt("pool", yo[s], yo[s], xs2[s], ALU.add, ["yo%d" % s, "xs2_%d" % s], ["yo%d" % s])
        dma("sp", out_d[tok, :], yo[s], "out%d" % s, ["yo%d" % s], ["outd%d" % s])
    P.op("sp", lambda e: None, ["outd0", "outd1"], [])
    return finish(nc, P, stack, out_d, None)


def finish(nc, P, stack, out_d, last):
    with stack:
        P.emit(nc, stack)
    return nc, P


def _prep_inputs(inputs):
    x = np.ascontiguousarray(inputs["x"], dtype=np.float32)
    mem = np.ascontiguousarray(inputs["mem"], dtype=np.float32)
    shared = {k: np.ascontiguousarray(inputs[k], dtype=np.float32) for k in (
        "norm_pre_g", "norm_post_g", "norm_mem_g", "w_in", "gla_a_w2", "gla_a_b", "gla_head_norm_g",
        "w_mem_kv", "w_proj_gla", "w_proj_sb", "w_proj_mem", "w_out")}
    zeros = np.zeros((TPRE, D), np.float32)
    in_maps = []
    for core in range(8):
        b, hf = core // 2, core % 2
        m = dict(shared)
        m["xo"] = np.ascontiguousarray(x[b, hf * 1024:(hf + 1) * 1024])
        m["xp"] = np.ascontiguousarray(x[b, 0:1024]) if hf == 1 else zeros
        m["memb"] = mem[b]
        in_maps.append(m)
    return in_maps


def kernel(**inputs):
    in_maps = _prep_inputs(inputs)
    nc, P = build_program()
    res = run_bass_kernel_spmd(nc, in_maps, core_ids=list(range(8)))
    out = np.empty((4, 2048, D), np.float32)
    for core in range(8):
        b, hf = core // 2, core % 2
        out[b, hf * 1024:(hf + 1) * 1024] = res.results[core]["out"]
    return out
```
